# Optimizing a Trainium2 kernel written in Bass

```python
import math
import jax, jax.numpy as jnp
from jax import lax
import numpy as np

D_MODEL = 1024
BATCH = 16
SEQ = 4096
DEPTH = 2
DEC_BATCH = 8
DEC_SEQ = 64
PAST_LEN = 4096

CHUNK = 64
Q_BLOCK = 128
A_HEADS = 6
A_QK = 32
A_V = 2 * A_QK
B_GROUPS = 4
B_CH = 64
POOL_WINDOWS = (2, 4, 8, 16)
POOL_HIST = max(POOL_WINDOWS) - 1
C_HEADS = 6
C_NOPE = 64
C_ROPE = 32
C_V = 64
C_Q_RANK = 256
C_KV_RANK = 128
ROPE_BASE = 10000.0
REL_BUCKETS = 32
REL_MAX_DIST = 128
FFN_DIM = 2816
CONV_W = 3
EPS = 1e-6

A_WIDTH = A_HEADS * A_V
B_WIDTH = B_GROUPS * B_CH
C_WIDTH = C_HEADS * C_V
MIX_WIDTH = A_WIDTH + B_WIDTH + C_WIDTH
OFF_AQ = 0
OFF_AK = OFF_AQ + A_HEADS * 2 * A_QK
OFF_AV = OFF_AK + A_HEADS * 2 * A_QK
OFF_B = OFF_AV + A_WIDTH
OFF_CQ = OFF_B + B_WIDTH
OFF_CKV = OFF_CQ + C_Q_RANK
OFF_CKR = OFF_CKV + C_KV_RANK
IN_COLS = OFF_CKR + C_ROPE

kernel_name = "hybrid_streaming_encoder_step"


def rmsnorm(x, g):
    xf = x.astype(jnp.float32)
    y = xf * lax.rsqrt(jnp.mean(xf * xf, axis=-1, keepdims=True) + EPS)
    return (y * g.astype(jnp.float32)).astype(x.dtype)


def apply_rope(x, pos):
    half = C_ROPE // 2
    inv = 1.0 / (ROPE_BASE ** (jnp.arange(half, dtype=jnp.float32) / half))
    ang = pos.astype(jnp.float32)[:, None] * inv[None, :]
    cos, sin = jnp.cos(ang), jnp.sin(ang)
    if x.ndim == 4:
        cos, sin = cos[:, None], sin[:, None]
    xf = x.astype(jnp.float32)
    x1, x2 = xf[..., :half], xf[..., half:]
    return jnp.concatenate([x1 * cos - x2 * sin, x2 * cos + x1 * sin], axis=-1).astype(x.dtype)


def t5_bucket(rel):
    half = REL_BUCKETS // 2
    exact = half // 2
    ret = jnp.where(rel > 0, half, 0)
    n = jnp.abs(rel)
    nf = jnp.maximum(n, 1).astype(jnp.float32)
    large = exact + (jnp.log(nf / exact) / math.log(REL_MAX_DIST / exact) * (half - exact)).astype(jnp.int32)
    large = jnp.minimum(large, half - 1)
    return ret + jnp.where(n < exact, n, large)


def chunk_mask(q_pos, k_pos):
    return (k_pos[None, :] // CHUNK) <= (q_pos[:, None] // CHUNK)


def map_query_blocks(fn, q, q_pos):
    b, t = q.shape[0], q.shape[1]
    if t <= Q_BLOCK or t % Q_BLOCK:
        return fn(q, q_pos)
    nb = t // Q_BLOCK
    qb = q.reshape(b, nb, Q_BLOCK, *q.shape[2:]).swapaxes(0, 1)
    pb = q_pos.reshape(nb, Q_BLOCK)
    out = lax.map(lambda a: fn(a[0], a[1]), (qb, pb))
    out = out.swapaxes(0, 1)
    return out.reshape(b, t, *out.shape[3:])


def diff_attention(q_in, k_in, v_in, past_k, past_v, q_pos, rel_bias, lq1, lk1, lq2, lk2, subln_g, layer_idx):
    b, t, _ = q_in.shape
    q = q_in.reshape(b, t, A_HEADS, 2 * A_QK)
    k = k_in.reshape(b, t, A_HEADS, 2 * A_QK)
    v = v_in.reshape(b, t, A_HEADS, A_V)
    k_all = k if past_k is None else jnp.concatenate([past_k, k], axis=1)
    v_all = v if past_v is None else jnp.concatenate([past_v, v], axis=1)
    k_pos = jnp.arange(k_all.shape[1], dtype=jnp.int32)
    lam_init = 0.8 - 0.6 * math.exp(-0.3 * layer_idx)
    f32 = jnp.float32
    lam = (jnp.exp(jnp.sum(lq1.astype(f32) * lk1.astype(f32)))
           - jnp.exp(jnp.sum(lq2.astype(f32) * lk2.astype(f32))) + lam_init)
    k1 = k_all[..., :A_QK].astype(f32)
    k2 = k_all[..., A_QK:].astype(f32)
    vf = v_all.astype(f32)
    scale = A_QK ** -0.5

    def block(qb, qpb):
        qf = qb.astype(f32)
        bias = rel_bias.astype(f32)[t5_bucket(k_pos[None, :] - qpb[:, None])]
        bias = bias.transpose(2, 0, 1)[None]
        mask = chunk_mask(qpb, k_pos)[None, None]
        s1 = jnp.einsum('bqhd,bkhd->bhqk', qf[..., :A_QK], k1) * scale + bias
        s2 = jnp.einsum('bqhd,bkhd->bhqk', qf[..., A_QK:], k2) * scale + bias
        p = (jax.nn.softmax(jnp.where(mask, s1, -jnp.inf), axis=-1)
             - lam * jax.nn.softmax(jnp.where(mask, s2, -jnp.inf), axis=-1))
        return jnp.einsum('bhqk,bkhd->bqhd', p, vf)

    o = map_query_blocks(block, q, q_pos)
    o = rmsnorm(o, subln_g) * (1.0 - lam_init)
    return o.reshape(b, t, A_WIDTH).astype(v_in.dtype), k, v


def multiscale_pool(u, hist, q_pos, pool_w, pool_scale):
    b, t, _ = u.shape
    if hist is None:
        hist = jnp.zeros((b, POOL_HIST, B_WIDTH), u.dtype)
    ext = jnp.concatenate([hist, u], axis=1)
    cs = jnp.concatenate([jnp.zeros((b, 1, B_WIDTH), jnp.float32),
                          jnp.cumsum(ext.astype(jnp.float32), axis=1)], axis=1)
    uf = u.astype(jnp.float32)
    outs = []
    for g, w in enumerate(POOL_WINDOWS):
        sl = slice(g * B_CH, (g + 1) * B_CH)
        tot = cs[:, POOL_HIST + 1:POOL_HIST + 1 + t, sl] - cs[:, POOL_HIST + 1 - w:POOL_HIST + 1 - w + t, sl]
        cnt = jnp.minimum(q_pos + 1, w).astype(jnp.float32)[None, :, None]
        outs.append(tot / cnt - uf[..., sl])
    m = jnp.stack(outs, axis=2)
    y = jnp.einsum('btgc,gcd->btgd', m, pool_w.astype(jnp.float32)).reshape(b, t, B_WIDTH)
    y = y * pool_scale.astype(jnp.float32)
    return y.astype(u.dtype), ext[:, -POOL_HIST:]


def latent_attention(cq_in, ckv_in, kr_in, past_lat, past_kr, q_pos, q_norm_g, w_q_up, kv_norm_g, w_kv_up):
    b, t, _ = cq_in.shape
    f32 = jnp.float32
    q = (rmsnorm(cq_in, q_norm_g) @ w_q_up).reshape(b, t, C_HEADS, C_NOPE + C_ROPE)
    q = jnp.concatenate([q[..., :C_NOPE], apply_rope(q[..., C_NOPE:], q_pos)], axis=-1)
    lat = rmsnorm(ckv_in, kv_norm_g)
    kr = apply_rope(kr_in, q_pos)
    lat_all = lat if past_lat is None else jnp.concatenate([past_lat, lat], axis=1)
    kr_all = kr if past_kr is None else jnp.concatenate([past_kr, kr], axis=1)
    tk = lat_all.shape[1]
    kv = (lat_all @ w_kv_up).reshape(b, tk, C_HEADS, C_NOPE + C_V)
    k_nope = kv[..., :C_NOPE].astype(f32)
    vf = kv[..., C_NOPE:].astype(f32)
    krf = kr_all.astype(f32)
    k_pos = jnp.arange(tk, dtype=jnp.int32)
    scale = (C_NOPE + C_ROPE) ** -0.5

    def block(qb, qpb):
        qf = qb.astype(f32)
        s = (jnp.einsum('bqhd,bkhd->bhqk', qf[..., :C_NOPE], k_nope)
             + jnp.einsum('bqhd,bkd->bhqk', qf[..., C_NOPE:], krf)) * scale
        mask = chunk_mask(qpb, k_pos)[None, None]
        p = jax.nn.softmax(jnp.where(mask, s, -jnp.inf), axis=-1)
        return jnp.einsum('bhqk,bkhd->bqhd', p, vf)

    o = map_query_blocks(block, q, q_pos)
    return o.reshape(b, t, C_WIDTH).astype(cq_in.dtype), lat, kr


def conv_ffn(h, hist, w_up, conv_w, conv_b, w_down):
    b, t, _ = h.shape
    up = h @ w_up
    g, val = up[..., :FFN_DIM], up[..., FFN_DIM:]
    if hist is None:
        hist = jnp.zeros((b, CONV_W - 1, FFN_DIM), g.dtype)
    ext = jnp.concatenate([hist, g], axis=1)
    gc = sum(conv_w[j] * ext[:, j:j + t] for j in range(CONV_W)) + conv_b
    out = (jax.nn.silu(gc) * val) @ w_down
    return out, ext[:, -(CONV_W - 1):]


def setup_inputs(seed: int = 0) -> dict:
    key = jax.random.key(seed)
    ks = iter(jax.random.split(key, 40))
    D = D_MODEL

    def nrm(shape, s=1.0):
        return jax.random.normal(next(ks), shape, jnp.float32) * s

    def gain(shape):
        return 1.0 + nrm(shape, 0.02)

    return {
        "x_prompt": nrm((BATCH, SEQ, D)),
        "x_sample": nrm((DEC_BATCH, DEC_SEQ, D)),
        "c_prompt": nrm((BATCH, D)),
        "c_sample": nrm((DEC_BATCH, D)),
        "cache_a_k": nrm((DEPTH, DEC_BATCH, PAST_LEN, A_HEADS, 2 * A_QK)),
        "cache_a_v": nrm((DEPTH, DEC_BATCH, PAST_LEN, A_HEADS, A_V)),
        "cache_c_latent": nrm((DEPTH, DEC_BATCH, PAST_LEN, C_KV_RANK)),
        "cache_c_krope": nrm((DEPTH, DEC_BATCH, PAST_LEN, C_ROPE)),
        "state_b_pool": nrm((DEPTH, DEC_BATCH, POOL_HIST, B_WIDTH)),
        "state_ffn_conv": nrm((DEPTH, DEC_BATCH, CONV_W - 1, FFN_DIM)),
        "w_ada": nrm((DEPTH, D, 6 * D), D ** -0.5),
        "b_ada": nrm((DEPTH, 6 * D), 0.02),
        "g_mix": gain((DEPTH, D)),
        "w_in": nrm((DEPTH, D, IN_COLS), D ** -0.5),
        "lam_q1": nrm((DEPTH, A_QK), 0.1),
        "lam_k1": nrm((DEPTH, A_QK), 0.1),
        "lam_q2": nrm((DEPTH, A_QK), 0.1),
        "lam_k2": nrm((DEPTH, A_QK), 0.1),
        "a_subln_g": gain((DEPTH, A_V)),
        "rel_bias": nrm((REL_BUCKETS, A_HEADS), 0.5),
        "pool_w": nrm((DEPTH, B_GROUPS, B_CH, B_CH), B_CH ** -0.5),
        "pool_scale": gain((DEPTH, B_WIDTH)),
        "c_q_norm_g": gain((DEPTH, C_Q_RANK)),
        "w_q_up": nrm((DEPTH, C_Q_RANK, C_HEADS * (C_NOPE + C_ROPE)), C_Q_RANK ** -0.5),
        "c_kv_norm_g": gain((DEPTH, C_KV_RANK)),
        "w_kv_up": nrm((DEPTH, C_KV_RANK, C_HEADS * (C_NOPE + C_V)), C_KV_RANK ** -0.5),
        "w_out": nrm((DEPTH, MIX_WIDTH, D), MIX_WIDTH ** -0.5),
        "g_ffn": gain((DEPTH, D)),
        "w_up": nrm((DEPTH, D, 2 * FFN_DIM), D ** -0.5),
        "conv_w": nrm((DEPTH, CONV_W, FFN_DIM), CONV_W ** -0.5),
        "conv_b": nrm((DEPTH, FFN_DIM), 0.01),
        "w_down": nrm((DEPTH, FFN_DIM, D), FFN_DIM ** -0.5),
        "g_final": gain((D,)),
    }


def reference(x_prompt, x_sample, c_prompt, c_sample, cache_a_k, cache_a_v, cache_c_latent, cache_c_krope,
              state_b_pool, state_ffn_conv, w_ada, b_ada, g_mix, w_in, lam_q1, lam_k1, lam_q2, lam_k2,
              a_subln_g, rel_bias, pool_w, pool_scale, c_q_norm_g, w_q_up, c_kv_norm_g, w_kv_up, w_out,
              g_ffn, w_up, conv_w, conv_b, w_down, g_final):

    def run_layer(l, x, c, q_pos, past):
        pa_k, pa_v, p_lat, p_kr, p_pool, p_conv = past
        mod = jax.nn.silu(c) @ w_ada[l] + b_ada[l]
        sh1, sc1, gt1, sh2, sc2, gt2 = jnp.split(mod[:, None, :], 6, axis=-1)
        h = rmsnorm(x, g_mix[l]) * (1 + sc1) + sh1
        u = h @ w_in[l]
        a_out, a_k, a_v = diff_attention(
            u[..., OFF_AQ:OFF_AK], u[..., OFF_AK:OFF_AV], u[..., OFF_AV:OFF_B], pa_k, pa_v, q_pos,
            rel_bias, lam_q1[l], lam_k1[l], lam_q2[l], lam_k2[l], a_subln_g[l], l)
        b_out, b_hist = multiscale_pool(u[..., OFF_B:OFF_CQ], p_pool, q_pos, pool_w[l], pool_scale[l])
        c_out, c_lat, c_kr = latent_attention(
            u[..., OFF_CQ:OFF_CKV], u[..., OFF_CKV:OFF_CKR], u[..., OFF_CKR:IN_COLS], p_lat, p_kr, q_pos,
            c_q_norm_g[l], w_q_up[l], c_kv_norm_g[l], w_kv_up[l])
        mix = jnp.concatenate([a_out, b_out, c_out], axis=-1) @ w_out[l]
        x = x + gt1 * mix
        h = rmsnorm(x, g_ffn[l]) * (1 + sc2) + sh2
        f, conv_hist = conv_ffn(h, p_conv, w_up[l], conv_w[l], conv_b[l], w_down[l])
        x = x + gt2 * f
        return x, (a_k, a_v, c_lat, c_kr, b_hist, conv_hist)

    pos_p = jnp.arange(x_prompt.shape[1], dtype=jnp.int32)
    pos_s = cache_a_k.shape[2] + jnp.arange(x_sample.shape[1], dtype=jnp.int32)
    hp, hs = x_prompt, x_sample
    new_p, new_s = [], []
    for l in range(DEPTH):
        hp, st_p = run_layer(l, hp, c_prompt, pos_p, (None, None, None, None, None, None))
        new_p.append(st_p)
        hs, st_s = run_layer(l, hs, c_sample, pos_s,
                             (cache_a_k[l], cache_a_v[l], cache_c_latent[l], cache_c_krope[l],
                              state_b_pool[l], state_ffn_conv[l]))
        new_s.append(st_s)
    y_prompt = rmsnorm(hp, g_final)
    y_sample = rmsnorm(hs, g_final)
    a_k_p, a_v_p, lat_p, kr_p, pool_p, conv_p = [jnp.stack(z, axis=0) for z in zip(*new_p)]
    a_k_s, a_v_s, lat_s, kr_s, pool_s, conv_s = [jnp.stack(z, axis=0) for z in zip(*new_s)]
    return (y_prompt, y_sample, a_k_p, a_v_p, lat_p, kr_p, pool_p, conv_p,
            a_k_s, a_v_s, lat_s, kr_s, pool_s, conv_s)
```

```python
import math
from contextlib import ExitStack
import numpy as np
import jax
import jax.numpy as jnp
import concourse.bass as bass
import concourse.mybir as mybir
from concourse.bass_utils import run_bass_kernel_spmd

F32 = mybir.dt.float32
BF16 = mybir.dt.bfloat16
AF = mybir.ActivationFunctionType
ALU = mybir.AluOpType
AX = mybir.AxisListType

D = 1024; T = 4096; DEPTH = 2; PAST = 4096; TS = 64
FFN = 2816; NFC = 22
EPS = 1e-6
NB = 256
TKV = PAST + TS
NKT = 33
VL = 160
SC_A = 32 ** -0.5
SC_C = 96 ** -0.5
SEM_LIMIT = 48000

class Sem:
    def __init__(self, h, idx):
        self.h = h; self.idx = idx; self.cnt = 0

class Ev:
    __slots__ = ("sem", "val", "clock")
    def __init__(self, sem, val, clock):
        self.sem = sem; self.val = val; self.clock = clock

class Buf:
    __slots__ = ("name", "w", "r", "excl")
    def __init__(self, name):
        self.name = name; self.w = None; self.r = {}
        self.excl = name.startswith("ps")

class Eng:
    def __init__(self, rec, name, sem, self_sync):
        self.rec = rec; self.name = name; self.sem = sem
        self.seen = {}
        self.ops = []
        self.self_sync = self_sync

    def wait(self, ev):
        if ev is None:
            return
        if self.sem is not None and ev.sem is self.sem and not self.self_sync:
            return
        if self.seen.get(ev.sem.idx, 0) >= ev.val:
            return
        h, v = ev.sem.h, ev.val
        self.ops.append(("wait", h, v))
        for k, c in ev.clock.items():
            if self.seen.get(k, 0) < c:
                self.seen[k] = c
        self.seen[ev.sem.idx] = v

    def _deps(self, reads, writes):
        for b in reads:
            self.wait(b.w)
            if b.excl:
                for e in list(b.r.values()):
                    if e.sem is not self.sem:
                        self.wait(e)
        for b in writes:
            self.wait(b.w)
            for e in list(b.r.values()):
                self.wait(e)

    def _reg(self, ev, reads, writes):
        for b in reads:
            b.r[ev.sem.idx] = ev
        for b in writes:
            b.w = ev; b.r = {}

    def op(self, fn, reads=(), writes=()):
        if self.sem.cnt >= SEM_LIMIT:
            old = self.sem
            self.ops.append(("wait", old.h, old.cnt))
            self.seen[old.idx] = old.cnt
            self.sem = self.rec.new_sem(self.name)
        self._deps(reads, writes)
        self.sem.cnt += 1
        self.ops.append(("op", fn, self.sem.h, 1))
        ev = Ev(self.sem, self.sem.cnt, dict(self.seen))
        self._reg(ev, reads, writes)
        return ev

    def dma(self, out, in_, reads=(), writes=(), **kw):
        self._deps(reads, writes)
        s = self.rec.next_dma_sem(self.name)
        if s.cnt > 0:
            self.wait(Ev(s, s.cnt * 16, {}))
        s.cnt += 1
        self.ops.append(("op", (lambda e, o=out, i=in_, k=kw: e.dma_start(out=o, in_=i, **k)), s.h, 16))
        ev = Ev(s, s.cnt * 16, dict(self.seen))
        self._reg(ev, reads, writes)
        return ev

class Rec:
    def __init__(self, nc, es, n_dma_sems=80):
        self.nc = nc
        self.es = es
        self.nsem = 0
        def mk(name, idx=None):
            return self.new_sem(name)
        self.pe = Eng(self, "pe", mk("s_pe", 0), False)
        self.act = Eng(self, "act", mk("s_act", 1), True)
        self.dve = Eng(self, "dve", mk("s_dve", 2), True)
        self.pool = Eng(self, "pool", mk("s_pool", 3), True)
        self.sync = Eng(self, "sync", None, False)
        self.dsems = [mk("s_d%d" % i, 4 + i) for i in range(n_dma_sems)]
        self.dpool = {"pool": self.dsems[0:28], "sync": self.dsems[28:]}
        self.dpi = {"pool": 0, "sync": 0}
        self.engs = [self.pe, self.act, self.dve, self.pool, self.sync]

    def new_sem(self, name):
        k = self.nsem; self.nsem += 1
        return Sem(self.es.enter_context(self.nc.semaphore("%s_%d" % (name, k))), k)

    def next_dma_sem(self, qname):
        lst = self.dpool[qname]
        s = lst[self.dpi[qname]]
        self.dpi[qname] = (self.dpi[qname] + 1) % len(lst)
        return s

    def barrier(self):
        for e in self.engs:
            for f in (self.pe, self.act, self.dve, self.pool):
                if f is not e and f.sem.cnt > 0:
                    e.wait(Ev(f.sem, f.sem.cnt, {}))
            for s in self.dsems:
                if s.cnt > 0:
                    e.wait(Ev(s, s.cnt * 16, {}))

    def replay(self, block):
        def run(eng):
            def f(h):
                for o in eng.ops:
                    if o[0] == "wait":
                        h.wait_ge(o[1], o[2])
                    else:
                        o[1](h).then_inc(o[2], o[3])
            return f
        block.tensor(run(self.pe))
        block.scalar(run(self.act))
        block.vector(run(self.dve))
        block.gpsimd(run(self.pool))
        block.sync(run(self.sync))


def _t5_bucket(rel):
    half = 16; exact = 8
    ret = jnp.where(rel > 0, half, 0)
    n = jnp.abs(rel)
    nf = jnp.maximum(n, 1).astype(jnp.float32)
    large = exact + (jnp.log(nf / exact) / math.log(128 / exact) * (half - exact)).astype(jnp.int32)
    large = jnp.minimum(large, half - 1)
    return ret + jnp.where(n < exact, n, large)

def _const_tables():
    with jax.default_device(jax.devices("cpu")[0]):
        delta = jnp.arange(383, dtype=jnp.int32) - 255
        bk = np.asarray(_t5_bucket(delta))
        ohv = np.zeros((32, 384), np.float32)
        ohv[bk, np.arange(383)] = 1.0
        half = 16
        inv = 1.0 / (10000.0 ** (jnp.arange(half, dtype=jnp.float32) / half))
        pos = jnp.arange(TKV, dtype=jnp.int32)
        ang = pos.astype(jnp.float32)[:, None] * inv[None, :]
        cos = np.asarray(jnp.cos(ang)); sin = np.asarray(jnp.sin(ang))
    cc = np.concatenate([cos, cos], 1)
    ss = np.concatenate([-sin, sin], 1)
    ropeF = np.zeros((96, 2, TKV), np.float32)
    for r in range(3):
        ropeF[r * 32:(r + 1) * 32, 0] = cc.T
        ropeF[r * 32:(r + 1) * 32, 1] = ss.T
    ropeT = np.zeros((128, NKT, 2, 32), np.float32)
    for kt in range(NKT):
        n = min(128, TKV - kt * 128)
        ropeT[:n, kt, 0] = cc[kt * 128:kt * 128 + n]
        ropeT[:n, kt, 1] = ss[kt * 128:kt * 128 + n]
    wins = (2, 4, 8, 16)
    invcnt0 = np.zeros((128, 2, NB), np.float32)
    invw = np.zeros((128, 2), np.float32)
    for c in range(2):
        for p in range(128):
            w = wins[2 * c + p // 64]
            invw[p, c] = np.float32(1.0) / np.float32(w)
            invcnt0[p, c] = np.float32(1.0) / np.minimum(np.arange(NB) + 1, w).astype(np.float32)
    ident = np.eye(128, dtype=np.float32)
    jmat = np.ascontiguousarray(ident[::-1])
    return dict(ohv=ohv, ropeF=ropeF, ropeT=ropeT, invcnt0=invcnt0, invw=invw, ident=ident, jmat=jmat)


def _kchunk(w):
    K, n = w.shape
    return np.ascontiguousarray(w.reshape(K // 128, 128, n).transpose(1, 0, 2))

def _fm(v):
    return np.ascontiguousarray(v.reshape(-1, 128).T)


SEQ_T = [T, T, TS]
STATS = {}
DEBUG_STOP = None

def build_program():
    nc = bass.Bass("TRN2", target_bir_lowering=False)
    def din(name, shape, dt=F32):
        return nc.dram_tensor(name, list(shape), dt, kind="ExternalInput")
    def dout(name, shape):
        return nc.dram_tensor(name, list(shape), F32, kind="ExternalOutput")
    def dscr(name, shape, dt):
        return nc.dram_tensor(name, list(shape), dt, kind="Internal")

    xT = [din("xT0", [128, 8, T]), din("xT1", [128, 8, T]), din("xT2", [128, 8, TS])]
    cT = din("cT", [128, 8, 3])
    wada = din("wada", [DEPTH, 128, 8, 6 * D])
    vecs = din("vecs", [128, 2 * VL + 8])
    rows = din("rows", [128, DEPTH * 192])
    lamv = din("lamv", [128, DEPTH * 4 * 32])
    relb = din("relb", [32, 6])
    r15c = din("r15c", [6, 1])
    r15r = din("r15r", [128, 6])
    ohv = din("ohv", [32, 384])
    ropeF = din("ropeF", [96, 2, TKV])
    ropeT = din("ropeT", [128, NKT, 2, 32])
    invcnt0 = din("invcnt0", [128, 2, NB])
    ident_in = din("ident", [128, 128])
    jmat_in = din("jmat", [128, 128])
    poolst = din("poolst", [DEPTH, 128, 2, 15])
    convst = din("convst", [DEPTH, 128, NFC, 2])
    cak = din("cak", [DEPTH, 128, 3, PAST])
    cav = din("cav", [DEPTH, 128, 6, 32, 65])
    clat = din("clat", [DEPTH, 128, PAST])
    ckr = din("ckr", [DEPTH, 96, PAST])
    WSH = dict(w_inF=[13 * 128, 8 * 128], w_inT=[128, 8 * 960], w_q=[128, 2 * 768], w_kv=[128, 768],
               w_pool=[128, 256], w_out=[8 * 128, 8 * 128], w_up=[11 * 128, 4096], w_down=[11 * 128, 2048])
    win = {k: din(k, [DEPTH] + v) for k, v in WSH.items()}
    wsc = {k: dscr(k + "_bf", [DEPTH] + v, BF16) for k, v in WSH.items()}

    yT = [dout("yT0", [128, 8, T]), dout("yT1", [128, 8, T]), dout("yT2", [128, 8, TS])]
    o_ak = [dout("ak%d" % s, [DEPTH, SEQ_T[s], 384]) for s in range(3)]
    o_av = [dout("av%d" % s, [DEPTH, SEQ_T[s], 384]) for s in range(3)]
    o_lat = [dout("lat%d" % s, [DEPTH, SEQ_T[s], 128]) for s in range(3)]
    o_kr = [dout("kr%d" % s, [DEPTH, SEQ_T[s], 32]) for s in range(3)]
    o_pool = [dout("pool%d" % s, [DEPTH, 128, 2, 15]) for s in range(3)]
    o_conv = [dout("conv%d" % s, [DEPTH, 128, NFC, 2]) for s in range(3)]

    xscr = dscr("xscr", [128, 8, T], F32)
    fscr = dscr("fscr", [6, 384], F32)
    KdT = dscr("KdT", [128, 3, TKV], BF16)
    KcT = dscr("KcT", [96, 6, TKV], BF16)
    KRs = dscr("KRs", [96, TKV], BF16)
    Vds = dscr("Vds", [128, 6, NKT, 65], BF16)
    Vcs = dscr("Vcs", [128, 6, NKT, 65], BF16)

    with ExitStack() as es:
        R = Rec(nc, es)
        PE, ACT, DVE, POOL, SY = R.pe, R.act, R.dve, R.pool, R.sync
        def sb(name, shape, dt=F32):
            nb = int(np.prod(shape[1:])) * (2 if dt == BF16 else 4)
            STATS.setdefault("sbuf", {})[name] = nb
            return es.enter_context(nc.sbuf_tensor(name, list(shape), dt))
        bufs = {}
        def B(name):
            if name not in bufs:
                bufs[name] = Buf(name)
            return bufs[name]

        def mm(out, lhsT, rhs, start, stop, reads, writes, tp=None):
            if tp is None:
                PE.op(lambda e: e.matmul(out, lhsT, rhs, start=start, stop=stop), reads, writes)
            else:
                PE.op(lambda e: e.matmul(out, lhsT, rhs, start=start, stop=stop, tile_position=tp), reads, writes)
        def actf(out, in_, func, reads, writes, bias=0.0, scale=1.0, accum=None):
            if accum is None:
                ACT.op(lambda e: e.activation(out=out, in_=in_, func=func, bias=bias, scale=scale), reads, writes)
            else:
                ACT.op(lambda e: e.activation(out=out, in_=in_, func=func, bias=bias, scale=scale, accum_out=accum), reads, writes)
        def tt(out, a, b, op, reads, writes, eng=None):
            (eng or DVE).op(lambda e: e.tensor_tensor(out=out, in0=a, in1=b, op=op), reads, writes)
        def ts(out, a, s1, s2, op0, op1, reads, writes, eng=None):
            if s2 is None:
                (eng or DVE).op(lambda e: e.tensor_single_scalar(out=out, in_=a, scalar=s1, op=op0), reads, writes)
            else:
                (eng or DVE).op(lambda e: e.tensor_scalar(out=out, in0=a, scalar1=s1, scalar2=s2, op0=op0, op1=op1), reads, writes)
        def stt(out, a, s, b, op0, op1, reads, writes, eng=None):
            (eng or DVE).op(lambda e: e.scalar_tensor_tensor(out=out, in0=a, scalar=s, in1=b, op0=op0, op1=op1), reads, writes)
        def cpy(out, in_, reads, writes, eng=None):
            (eng or DVE).op(lambda e: e.tensor_copy(out=out, in_=in_), reads, writes)
        def rstd_from(out, ss_ap, n, reads, writes):
            actf(out, ss_ap, AF.Ln, reads, writes, bias=EPS, scale=1.0 / n)
            actf(out, out, AF.Exp, writes, writes, scale=-0.5)

        PS = [es.enter_context(nc.psum_tensor("ps%d" % i, [128, 512], F32)) for i in range(4)]
        SS = [es.enter_context(nc.psum_tensor("pss%d" % i, [128, 1024], F32)) for i in range(2)]
        class Rot:
            def __init__(self, items, names):
                self.items = items; self.names = names; self.i = 0
            def next(self):
                k = self.i; self.i = (self.i + 1) % len(self.items)
                return self.items[k], B(self.names[k])
        rD = Rot(PS[0:2], ["psD0", "psD1"])
        rS = Rot(SS, ["psS0", "psS1"])
        rA = Rot(PS[2:4], ["psA0", "psA1"])
        DACC = [(SS[0], B("psS0")), (SS[1], B("psS1"))]
        rF = Rot(PS[0:4], ["psD0", "psD1", "psA0", "psA1"])

        ident_bf = sb("ident_bf", [128, 128], BF16)
        ident_f = sb("ident_f", [128, 128])
        ones_bf = sb("ones_bf", [128, 128], BF16)
        BIAST = sb("BIAST", [128, 2, 6, 128], BF16)
        VEC = sb("VEC", [128, 2 * VL + 8])
        ROWS = sb("ROWS", [128, DEPTH * 192])
        R15 = sb("R15", [128, 6])
        LAM = sb("LAM", [128, 4])
        PRM = sb("PRM", [128, DEPTH, 3, 6, 8])
        INVC0 = sb("INVC0", [128, 2, NB])

        SY.dma(ident_f[:], ident_in.ap(), [], [B("ident_f")])
        POOL.dma(ident_bf[:], ident_in.ap(), [], [B("ident_bf")])
        SY.dma(VEC[:], vecs.ap(), [], [B("VEC")])
        SY.dma(ROWS[:], rows.ap(), [], [B("ROWS")])
        SY.dma(R15[:], r15r.ap(), [], [B("R15")])
        SY.dma(INVC0[:], invcnt0.ap(), [], [B("INVC0")])
        DVE.op(lambda e: e.memset(ones_bf[:], 1.0), [], [B("ones_bf")])
        ZR = sb("ZR", [128, 512], BF16)
        DVE.op(lambda e: e.memset(ZR[:], 0.0), [], [B("ZR")])

        for l in range(DEPTH):
            for k in WSH:
                nr = WSH[k][0]
                for r0 in range(0, nr, 128):
                    POOL.dma(wsc[k].ap()[l, r0:r0 + 128, :], win[k].ap()[l, r0:r0 + 128, :], [], [B("ws_%s_%d_%d" % (k, l, r0 // 128))])

        with ExitStack() as es2:
            def sb2(name, shape, dt=F32):
                return es2.enter_context(nc.sbuf_tensor(name, list(shape), dt))
            WA = [sb2("WA0", [128, 8, 512]), sb2("WA1", [128, 8, 512])]
            CS = sb2("CS", [128, 8, 3]); SCF = sb2("SCF", [128, 8, 3])
            MODT = sb2("MODT", [128, DEPTH, 48, 3])
            LV = sb2("LV", [128, DEPTH * 4 * 32]); LT = sb2("LT", [128, 32]); LS = sb2("LS", [128, 4])
            RELB = sb2("RELB", [32, 6]); OHV = sb2("OHV", [32, 384]); R15C = sb2("R15C", [6, 1])
            FSB = sb2("FSB", [6, 384]); HSB = sb2("HSB", [128, 2, 6, 128]); JM = sb2("JM", [128, 128])
            TMPP = sb2("TMPP", [128, 8])

            SY.dma(CS[:], cT.ap(), [], [B("CS")])
            SY.dma(LV[:], lamv.ap(), [], [B("LV")])
            SY.dma(RELB[:], relb.ap(), [], [B("RELB")])
            SY.dma(OHV[:], ohv.ap(), [], [B("OHV")])
            SY.dma(R15C[:], r15c.ap(), [], [B("R15C")])
            SY.dma(JM[:], jmat_in.ap(), [], [B("JM")])
            actf(SCF[:], CS[:], AF.Silu, [B("CS")], [B("SCF")])
            for l in range(DEPTH):
                mps, mpb = rD.next()
                for pc in range(12):
                    wa = WA[pc % 2]; wab = B("WA%d" % (pc % 2))
                    SY.dma(wa[:], wada.ap()[l, :, :, pc * 512:(pc + 1) * 512], [], [wab])
                    for j in range(4):
                        ch = pc * 4 + j
                        for kc in range(8):
                            mm(mps[:, ch * 3:ch * 3 + 3], wa[:, kc, j * 128:(j + 1) * 128], SCF[:, kc, :],
                               kc == 0, kc == 7, [wab, B("SCF")], [mpb])
                boff = l * VL + 112
                for s in range(3):
                    tt(MODT[:, l, :, s], mps[:, 0:144].rearrange("p (c s) -> p c s", s=3)[:, :, s],
                       VEC[:, boff:boff + 48], ALU.add, [mpb, B("VEC")], [B("MODT")])
                go = l * VL
                for s in range(3):
                    for (dst, scj, gof) in ((0, 8, 0), (3, 32, 8)):
                        ts(TMPP[:], MODT[:, l, scj:scj + 8, s], 1.0, None, ALU.add, None, [B("MODT")], [B("TMPP")])
                        tt(PRM[:, l, s, dst, :], TMPP[:], VEC[:, go + gof:go + gof + 8], ALU.mult,
                           [B("TMPP"), B("VEC")], [B("PRM")])
                    for (dst, j0) in ((1, 0), (2, 16), (4, 24), (5, 40)):
                        cpy(PRM[:, l, s, dst, :], MODT[:, l, j0:j0 + 8, s], [B("MODT")], [B("PRM")])
            for l in range(DEPTH):
                for i in range(2):
                    o = (l * 4 + 2 * i) * 32
                    tt(LT[:], LV[:, o:o + 32], LV[:, o + 32:o + 64], ALU.mult, [B("LV")], [B("LT")])
                    DVE.op(lambda e, a=LS[:, l * 2 + i:l * 2 + i + 1]: e.reduce_sum(out=a, in_=LT[:], axis=AX.X),
                           [B("LT")], [B("LS")])
            actf(LS[:], LS[:], AF.Exp, [B("LS")], [B("LS")])
            for l in range(DEPTH):
                lam_init = 0.8 - 0.6 * math.exp(-0.3 * l)
                tt(LAM[:, l:l + 1], LS[:, 2 * l:2 * l + 1], LS[:, 2 * l + 1:2 * l + 2], ALU.subtract, [B("LS")], [B("LAM")])
                ts(LAM[:, l:l + 1], LAM[:, l:l + 1], lam_init, None, ALU.add, None, [B("LAM")], [B("LAM")])
                ts(LAM[:, 2 + l:3 + l], LAM[:, l:l + 1], -1.0, None, ALU.mult, None, [B("LAM")], [B("LAM")])
            fps, fpb = rD.next()
            mm(fps[0:6, 0:384], RELB[:, :], OHV[:, :], True, True, [B("RELB"), B("OHV")], [fpb])
            ts(FSB[:], fps[0:6, 0:384], R15C[:, 0:1], 1.0 / SC_A, ALU.subtract, ALU.mult, [fpb, B("R15C")], [B("FSB")])
            SY.dma(fscr.ap(), FSB[:], [B("FSB")], [B("fscr")])
            for h in range(6):
                for d in range(2):
                    src = bass.AP(tensor=fscr, offset=h * 384 + (128 if d == 0 else 0), ap=[[1, 128], [1, 128]])
                    SY.dma(HSB[:, d, h, :], src, [B("fscr")], [B("HSB")])
            for d in range(2):
                for h in range(6):
                    bp, bpb = rD.next()
                    mm(bp[:, 0:128], HSB[:, d, h, :], JM[:, :], True, True, [B("HSB"), B("JM")], [bpb])
                    actf(BIAST[:, d, h, :], bp[:, 0:128], AF.Copy, [bpb], [B("BIAST")])
            R.barrier()
            with nc.Block() as blockctx0:
                R.replay(blockctx0)
            for e_ in R.engs:
                e_.ops = []
        XTS = [sb("XTa", [128, 8, NB]), sb("XTb", [128, 8, NB])]
        xsel = [0]
        HT = sb("HT", [128, 8, NB], BF16); SQ = sb("SQ", [128, 8, NB], BF16)
        MIXT = sb("MIXT", [128, 8, NB], BF16)
        TMP = [sb("TMP%d" % i, [128, NB]) for i in range(2)]
        rTMP = Rot(TMP, ["TMP0", "TMP1"])
        RSTD = sb("RSTD", [128, NB]); RSTDQ = sb("RSTDQ", [128, NB]); RSTDKV = sb("RSTDKV", [128, NB])
        WF = [sb("WF%d" % i, [128, 8, 128], BF16) for i in range(4)]
        rWF = Rot(WF, ["WF%d" % i for i in range(4)])
        WT = sb("WT", [128, 8, 960], BF16)
        WQ = sb("WQ", [128, 2, 768], BF16); WKV = sb("WKV", [128, 768], BF16); WPOOL = sb("WPOOL", [128, 2, 128], BF16)
        QDM = sb("QDM", [128, 3, 4, NB], BF16); KST = sb("KST", [128, 3, NB], BF16)
        UB = sb("UB", [128, 2, 15 + NB])
        CQG = sb("CQG", [128, 2, NB], BF16); CQSQ = sb("CQSQ", [128, 2, NB], BF16)
        CKVSQ = sb("CKVSQ", [128, NB], BF16); CKVG = sb("CKVG", [128, NB])
        T1 = sb("T1", [128, NB]); T2 = sb("T2", [128, NB])
        KRST = sb("KRST", [96, NB], BF16); LATT = sb("LATT", [128, NB], BF16)
        OSB = [sb("OSB%d" % i, [128, 928]) for i in range(2)]
        rOSB = Rot(OSB, ["OSB0", "OSB1"])
        JUNK = sb("JUNK", [128, 128]); SSL = sb("SSL", [128, 2]); KT1 = sb("KT1", [128, 32]); KT2 = sb("KT2", [128, 32])
        VST = sb("VST", [128, 6, 2, 65], BF16); VCST = sb("VCST", [128, 6, 2, 65], BF16)
        KCST = sb("KCST", [96, 6, NB], BF16)
        QC = sb("QC", [128, 6, NB], BF16)
        PL = [sb("PL%d" % i, [128, 15 + NB]) for i in range(4)]
        MB = sb("MB", [128, 2, NB], BF16)
        KB = [sb("KB%d" % i, [128, TKV], BF16) for i in range(2)]
        rKB = Rot(KB, ["KB0", "KB1"])
        VB = [sb("VB%d" % i, [128, NKT, 65], BF16) for i in range(2)]
        rVB = Rot(VB, ["VB0", "VB1"])
        PT = [sb("PT%d" % i, [128, 2 * NB], BF16) for i in range(3)]
        rPT = Rot(PT, ["PT0", "PT1", "PT2"])
        OT = [sb("OT%d" % i, [65, 2 * NB]) for i in range(2)]
        rOT = Rot(OT, ["OT0", "OT1"])
        OM = [sb("OM%d" % i, [128, 2, 65]) for i in range(3)]
        RR = sb("RR", [128, 8]); DD = sb("DD", [128, 2, 64]); SSD = sb("SSD", [128, 2]); RSD = sb("RSD", [128, 2])
        MIXTOK = sb("MIXTOK", [128, 2, 768], BF16)
        WU = [sb("WU%d" % i, [128, 2, 2, 8, 128], BF16) for i in range(2)]
        rWU = Rot(WU, ["WU0", "WU1"])
        WD = [sb("WD%d" % i, [128, 2, 1024], BF16) for i in range(3)]
        rWD = Rot(WD, ["WD0", "WD1", "WD2"])
        GSB = [sb("GSB%d" % i, [128, NB + 2]) for i in range(2)]
        rGSB = Rot(GSB, ["GSB0", "GSB1"])
        AT = [sb("AT%d" % i, [128, NB]) for i in range(2)]
        rAT = Rot(AT, ["AT0", "AT1"])
        STt = [sb("ST%d" % i, [128, NB]) for i in range(2)]
        rST = Rot(STt, ["ST0", "ST1"])
        ACTB = [sb("ACTB%d" % i, [128, 2, NB], BF16) for i in range(3)]
        rACTB = Rot(ACTB, ["ACTB0", "ACTB1", "ACTB2"])
        GCAR = sb("GCAR", [128, NFC, 2])
        RPF = sb("RPF", [96, 2, NB]); RPT = sb("RPT", [128, 2, 2, 32])
        YTb = [sb("YT%d" % i, [128, NB]) for i in range(8)]
        rYT = Rot(YTb, ["YT%d" % i for i in range(8)])

        DVE.op(lambda e: e.memset(QDM[:], 0.0), [], [B("QDM")])
        DVE.op(lambda e: e.memset(QC[:], 0.0), [], [B("QC")])
        DVE.op(lambda e: e.memset(VST[:, :, :, 64:65], 1.0), [], [B("VST")])
        DVE.op(lambda e: e.memset(VCST[:, :, :, 64:65], 1.0), [], [B("VCST")])

        out_events = []

        def norm(N, A_ap, B_ap):
            XT = XTS[xsel[0]]; XTN = "XT%d" % xsel[0]
            actf(SQ[:, :, 0:N], XT[:, :, 0:N], AF.Square, [B(XTN)], [B("SQ")])
            ps, pb = rD.next()
            for c in range(8):
                mm(ps[:, 0:N], ones_bf[:, :], SQ[:, c, 0:N], c == 0, c == 7, [B("SQ"), B("ones_bf")], [pb])
            rstd_from(RSTD[:, 0:N], ps[:, 0:N], float(D), [pb], [B("RSTD")])
            for c in range(8):
                tm, tb = rTMP.next()
                tt(tm[:, 0:N], XT[:, c, 0:N], RSTD[:, 0:N], ALU.mult, [B(XTN), B("RSTD")], [tb])
                actf(HT[:, c, 0:N], tm[:, 0:N], AF.Identity, [tb, B("PRM"), B("VEC")], [B("HT")],
                     bias=B_ap[:, c:c + 1], scale=A_ap[:, c:c + 1])

        def mla_kv_from_latt(l, t0, N, rowsl, with_kr=True):
            kt0 = t0 // 128
            for ti, rw in enumerate(rowsl):
                ps, pb = rD.next()
                mm(ps[0:rw, 0:384], LATT[:, ti * 128:ti * 128 + rw], WKV[:, 384:768], True, True,
                   [B("LATT"), B("WKV")], [pb])
                cpy(VCST[0:rw, :, ti, 0:64], ps[0:rw, 0:384].rearrange("p (h d) -> p h d", d=64), [pb], [B("VCST")])
            for hh in range(6):
                ps, pb = rD.next()
                mm(ps[0:64, 0:N], WKV[:, hh * 64:(hh + 1) * 64], LATT[:, 0:N], True, True, [B("LATT"), B("WKV")], [pb])
                actf(KCST[0:64, hh, 0:N], ps[0:64, 0:N], AF.Copy, [pb], [B("KCST")])
                if with_kr:
                    cpy(KCST[64:96, hh, 0:N], KRST[64:96, 0:N], [B("KRST")], [B("KCST")])
            if with_kr:
                POOL.dma(KcT.ap()[:, :, t0:t0 + N], KCST[:, :, 0:N], [B("KCST")], [B("KcT")])
            else:
                POOL.dma(KcT.ap()[0:64, :, t0:t0 + N], KCST[0:64, :, 0:N], [B("KCST")], [B("KcT")])
            POOL.dma(Vcs.ap()[:, :, kt0:kt0 + len(rowsl), :], VCST[:, :, 0:len(rowsl), :], [B("VCST")], [B("Vcs")])

        def load_small_weights(l):
            SY.dma(WQ[:], wsc["w_q"].ap()[l].rearrange("p (a b) -> p a b", a=2), [B("ws_w_q_%d_0" % l)], [B("WQ")])
            SY.dma(WKV[:], wsc["w_kv"].ap()[l], [B("ws_w_kv_%d_0" % l)], [B("WKV")])
            SY.dma(WPOOL[:], wsc["w_pool"].ap()[l].rearrange("p (a b) -> p a b", a=2), [B("ws_w_pool_%d_0" % l)], [B("WPOOL")])

        def attn_pass(kind, l, h, N, t0, kt0, nkt, last_ksz, is_sample, kb, kbb, vb, vbb, bg_step, bg_flush):
            TTq = max(1, N // 128)
            nm = 2 if kind == "a" else 1
            acc, accb = rA.next()
            scale = SC_A if kind == "a" else SC_C
            def v3(t, rows, c0, st=NB):
                if nm == 1:
                    return t[rows, c0:N]
                return t[rows, 0:2 * st].rearrange("p (m n) -> p m n", m=2)[:, :, c0:N]
            def qk(kt):
                ksz = last_ksz if kt == nkt - 1 else 128
                c0 = 0 if is_sample else 128 * max(0, kt - kt0)
                s_, sbf = rS.next()
                ks = slice(kt * 128, kt * 128 + ksz)
                groups = []
                if kind == "a":
                    for m in range(2):
                        p0 = 64 * (h % 2) + 32 * m
                        g = [(s_[0:ksz, m * 512 + c0:m * 512 + N], kb[:, ks], QDM[:, h // 2, 2 * (h % 2) + m, c0:N], [kbb, B("QDM")], None)]
                        for jq in range(TTq):
                            o = kt0 + jq - kt
                            if o in (0, 1) and jq * 128 >= c0:
                                qw = min(128, N)
                                g.append((s_[0:ksz, m * 512 + jq * 128:m * 512 + jq * 128 + qw], ident_bf[0:ksz, 0:ksz],
                                          BIAST[0:ksz, o, h, 0:qw], [B("ident_bf"), B("BIAST")], None))
                        groups.append(g)
                else:
                    groups.append([(s_[0:ksz, c0:N], kb[:, ks], QC[:, h, c0:N], [kbb, B("QC")], None)])
                depth = max(len(g) for g in groups)
                for i in range(depth):
                    for g in groups:
                        if i < len(g):
                            o_, a_, b_, rd, tp = g[i]
                            mm(o_, a_, b_, i == 0, i == len(g) - 1, rd, [sbf], tp=tp)
                return s_, sbf, ksz, c0
            def fin(kt, st):
                s_, sbf, ksz, c0 = st
                pt, ptb = rPT.next()
                rows = slice(0, ksz)
                if kind == "a":
                    actf(v3(pt, rows, c0), v3(s_, rows, c0, 512), AF.Exp, [sbf, B("R15")], [ptb], bias=R15[0:ksz, h:h + 1], scale=scale)
                else:
                    actf(v3(pt, rows, c0), v3(s_, rows, c0, 512), AF.Exp, [sbf], [ptb], scale=scale)
                if not is_sample and kt >= kt0:
                    jq = kt - kt0
                    if nm == 1:
                        msk = pt[64:128, jq * 128:jq * 128 + 64]
                    else:
                        msk = pt[64:128, 0:2 * NB].rearrange("p (m n) -> p m n", m=2)[:, :, jq * 128:jq * 128 + 64]
                    DVE.op(lambda e, a=msk: e.memset(a, 0.0), [], [ptb])
                mm(v3(acc, slice(0, 65), c0), vb[0:ksz, kt, :], v3(pt, rows, c0), kt == 0, kt == nkt - 1, [vbb, ptb], [accb])
            pend = []
            for kt in range(min(2, nkt)):
                pend.append((kt, qk(kt)))
            nxt = len(pend)
            done = 0
            yielded = False
            while pend:
                kt, st = pend.pop(0)
                fin(kt, st)
                if nxt < nkt:
                    pend.append((nxt, qk(nxt))); nxt += 1
                done += 1
                if done > 2:
                    bg_step(); bg_step()
                if done == 2 and not yielded:
                    yielded = True
                    yield None
            if not yielded:
                yield None
            bg_flush()
            def tail():
                rowsl = [128] * TTq if N >= 128 else [N]
                ot, otb = rOT.next()
                if nm == 1:
                    actf(ot[:, 0:N], acc[0:65, 0:N], AF.Copy, [accb], [otb])
                else:
                    actf(ot[:, 0:2 * NB].rearrange("p (m n) -> p m n", m=2)[:, :, 0:N],
                         acc[0:65, 0:2 * NB].rearrange("p (m n) -> p m n", m=2)[:, :, 0:N], AF.Copy, [accb], [otb])
                ps, pb = rD.next()
                for m in range(nm):
                    for ti, rw in enumerate(rowsl):
                        cc = (m * TTq + ti) * 65
                        mm(ps[0:rw, cc:cc + 65], ot[0:65, m * NB + ti * 128:m * NB + ti * 128 + rw], ident_f[0:65, 0:65], True, True,
                           [otb, B("ident_f")], [pb])
                for m in range(nm):
                    oi = m if kind == "a" else 2
                    for ti, rw in enumerate(rowsl):
                        cc = (m * TTq + ti) * 65
                        cpy(OM[oi][0:rw, ti, :], ps[0:rw, cc:cc + 65], [pb], [B("OM%d" % oi)])
            yield tail

        class _CkStop(Exception):
            pass
        def ck(n):
            if DEBUG_STOP is not None and DEBUG_STOP.get("ck") == n:
                raise _CkStop()

        def block(s, l, blk, N, t0, is_sample, is_last):
            TTq = max(1, N // 128)
            rowsl = [128] * TTq if N >= 128 else [N]
            kt0 = t0 // 128
            nkt = kt0 + TTq
            last_ksz = rowsl[-1]
            vo = l * VL
            P_ = lambda j: PRM[:, l, s, j, :]
            tq = t0 - (PAST if is_sample else 0)
            xsel[0] ^= 1
            XT = XTS[xsel[0]]; XTN = "XT%d" % xsel[0]
            if l == 0:
                SY.dma(XT[:, :, 0:N], xT[s].ap()[:, :, tq:tq + N], [], [B(XTN)])
            else:
                SY.dma(XT[:, :, 0:N], xscr.ap()[:, :, tq:tq + N], [B("xscr")], [B(XTN)])
            SY.dma(RPF[:, :, 0:N], ropeF.ap()[:, :, t0:t0 + N], [], [B("RPF")])
            SY.dma(RPT[:, 0:TTq, :, :], ropeT.ap()[:, kt0:kt0 + TTq, :, :], [], [B("RPT")])
            SY.dma(WT[:], wsc["w_inT"].ap()[l].rearrange("p (a b) -> p a b", a=8), [B("ws_w_inT_%d_0" % l)], [B("WT")])
            norm(N, P_(0), P_(1))
            ck(1)
            pend_kr = None
            for j in range(13):
                wf, wfb = rWF.next()
                SY.dma(wf[:], wsc["w_inF"].ap()[l, j * 128:(j + 1) * 128, :].rearrange("p (a b) -> p a b", a=8),
                       [B("ws_w_inF_%d_%d" % (l, j))], [wfb])
                M = 96 if j >= 11 else 128
                ps, pb = rD.next()
                for kc in range(8):
                    mm(ps[0:M, 0:N], wf[:, kc, 0:M], HT[:, kc, 0:N], kc == 0, kc == 7, [wfb, B("HT")], [pb])
                if j < 3:
                    for hm in range(4):
                        actf(QDM[32 * hm:32 * hm + 32, j, hm, 0:N], ps[32 * hm:32 * hm + 32, 0:N], AF.Copy, [pb], [B("QDM")])
                elif j < 6:
                    actf(KST[:, j - 3, 0:N], ps[:, 0:N], AF.Copy, [pb], [B("KST")])
                elif j < 8:
                    cpy(UB[:, j - 6, 15:15 + N], ps[:, 0:N], [pb], [B("UB")])
                elif j < 10:
                    c = j - 8
                    actf(CQG[:, c, 0:N], ps[:, 0:N], AF.Identity, [pb, B("VEC")], [B("CQG")], scale=VEC[:, vo + 18 + c:vo + 19 + c])
                    actf(CQSQ[:, c, 0:N], ps[:, 0:N], AF.Square, [pb], [B("CQSQ")])
                elif j == 10:
                    actf(CKVSQ[:, 0:N], ps[:, 0:N], AF.Square, [pb], [B("CKVSQ")])
                    ts(CKVG[:, 0:N], ps[:, 0:N], VEC[:, vo + 110:vo + 111], None, ALU.mult, None, [pb, B("VEC")], [B("CKVG")])
                elif j == 11:
                    tt(T1[0:96, 0:N], ps[0:96, 0:N], RPF[:, 0, 0:N], ALU.mult, [pb, B("RPF")], [B("T1")])
                else:
                    tt(T2[0:96, 0:N], ps[0:96, 0:N], RPF[:, 1, 0:N], ALU.mult, [pb, B("RPF")], [B("T2")])
                    tt(KRST[:, 0:N], T1[0:96, 0:N], T2[0:96, 0:N], ALU.add, [B("T1"), B("T2")], [B("KRST")])
                ck(100 + j)
            ck(2)
            ps, pb = rD.next()
            for c in range(2):
                mm(ps[:, 0:N], ones_bf[:, :], CQSQ[:, c, 0:N], c == 0, c == 1, [B("CQSQ"), B("ones_bf")], [pb])
            rstd_from(RSTDQ[:, 0:N], ps[:, 0:N], 256.0, [pb], [B("RSTDQ")])
            ps, pb = rD.next()
            mm(ps[:, 0:N], ones_bf[:, :], CKVSQ[:, 0:N], True, True, [B("CKVSQ"), B("ones_bf")], [pb])
            rstd_from(RSTDKV[:, 0:N], ps[:, 0:N], 128.0, [pb], [B("RSTDKV")])
            tt(LATT[:, 0:N], CKVG[:, 0:N], RSTDKV[:, 0:N], ALU.mult, [B("CKVG"), B("RSTDKV")], [B("LATT")])
            POOL.dma(KdT.ap()[:, :, t0:t0 + N], KST[:, :, 0:N], [B("KST")], [B("KdT")])
            ck(3)
            for ti, rw in enumerate(rowsl):
                tk = slice(ti * 128, ti * 128 + rw)
                psA, pbA = rD.next()
                for kc in range(8):
                    mm(psA[0:rw, 0:512], HT[:, kc, tk], WT[:, kc, 0:512], kc == 0, kc == 7, [B("HT"), B("WT")], [pbA])
                psB, pbB = rD.next()
                for kc in range(8):
                    mm(psB[0:rw, 0:448], HT[:, kc, tk], WT[:, kc, 512:960], kc == 0, kc == 7, [B("HT"), B("WT")], [pbB])
                osb, osbb = rOSB.next()
                actf(osb[0:rw, 0:384], psA[0:rw, 0:384], AF.Copy, [pbA], [osbb])
                actf(JUNK[0:rw, :], psA[0:rw, 384:512], AF.Square, [pbA], [B("JUNK")])
                DVE.op(lambda e, o=SSL[0:rw, 0:1], i=JUNK[0:rw, :]: e.reduce_sum(out=o, in_=i, axis=AX.X), [B("JUNK")], [B("SSL")])
                rstd_from(SSL[0:rw, 1:2], SSL[0:rw, 0:1], 128.0, [B("SSL")], [B("SSL")])
                stt(osb[0:rw, 768:896], psA[0:rw, 384:512], SSL[0:rw, 1:2], ROWS[0:rw, l * 192:l * 192 + 128], ALU.mult, ALU.mult,
                    [pbA, B("SSL"), B("ROWS")], [osbb])
                actf(osb[0:rw, 384:768], psB[0:rw, 0:384], AF.Copy, [pbB], [osbb])
                cpy(VST[0:rw, :, ti, 0:64], psB[0:rw, 0:384].rearrange("p (h d) -> p h d", d=64), [pbB], [B("VST")])
                tt(KT1[0:rw, :], psB[0:rw, 384:416], RPT[0:rw, ti, 0, :], ALU.mult, [pbB, B("RPT")], [B("KT1")])
                tt(KT2[0:rw, :], psB[0:rw, 416:448], RPT[0:rw, ti, 1, :], ALU.mult, [pbB, B("RPT")], [B("KT2")])
                tt(osb[0:rw, 896:928], KT1[0:rw, :], KT2[0:rw, :], ALU.add, [B("KT1"), B("KT2")], [osbb])
                r0 = t0 - (PAST if is_sample else 0) + ti * 128
                out_events.append(POOL.dma(o_ak[s].ap()[l, r0:r0 + rw, :], osb[0:rw, 0:384], [osbb], []))
                out_events.append(POOL.dma(o_av[s].ap()[l, r0:r0 + rw, :], osb[0:rw, 384:768], [osbb], []))
                out_events.append(POOL.dma(o_lat[s].ap()[l, r0:r0 + rw, :], osb[0:rw, 768:896], [osbb], []))
                out_events.append(POOL.dma(o_kr[s].ap()[l, r0:r0 + rw, :], osb[0:rw, 896:928], [osbb], []))
            POOL.dma(Vds.ap()[:, :, kt0:kt0 + TTq, :], VST[:, :, 0:TTq, :], [B("VST")], [B("Vds")])
            ck(4)
            mla_kv_from_latt(l, t0, N, rowsl)
            ck(5)
            for hh in range(6):
                ps, pb = rD.next()
                for kc in range(2):
                    mm(ps[0:96, 0:N], WQ[:, kc, hh * 96:(hh + 1) * 96], CQG[:, kc, 0:N], kc == 0, kc == 1, [B("WQ"), B("CQG")], [pb])
                ps2, pb2 = rD.next()
                for kc in range(2):
                    mm(ps2[64:96, 0:N], WQ[:, kc, 576 + hh * 32:576 + (hh + 1) * 32], CQG[:, kc, 0:N], kc == 0, kc == 1,
                       [B("WQ"), B("CQG")], [pb2], tp=(0, 64))
                tt(QC[0:64, hh, 0:N], ps[0:64, 0:N], RSTDQ[0:64, 0:N], ALU.mult, [pb, B("RSTDQ")], [B("QC")])
                tt(T1[64:96, 0:N], ps[64:96, 0:N], RPF[64:96, 0, 0:N], ALU.mult, [pb, B("RPF")], [B("T1")])
                tt(T2[64:96, 0:N], ps2[64:96, 0:N], RPF[64:96, 1, 0:N], ALU.mult, [pb2, B("RPF")], [B("T2")])
                tt(T1[64:96, 0:N], T1[64:96, 0:N], T2[64:96, 0:N], ALU.add, [B("T1"), B("T2")], [B("T1")])
                tt(QC[64:96, hh, 0:N], T1[64:96, 0:N], RSTDQ[64:96, 0:N], ALU.mult, [B("T1"), B("RSTDQ")], [B("QC")])
            ck(6)
            L = 15 + N
            for c in range(2):
                u = UB[:, c, :]
                tt(PL[0][:, 1:L], u[:, 1:L], u[:, 0:L - 1], ALU.add, [B("UB")], [B("PL0")])
                tt(PL[1][:, 3:L], PL[0][:, 3:L], PL[0][:, 1:L - 2], ALU.add, [B("PL0")], [B("PL1")])
                if c == 0:
                    srcs = [(PL[0], "PL0"), (PL[1], "PL1")]
                else:
                    tt(PL[2][:, 7:L], PL[1][:, 7:L], PL[1][:, 3:L - 4], ALU.add, [B("PL1")], [B("PL2")])
                    tt(PL[3][:, 15:L], PL[2][:, 15:L], PL[2][:, 7:L - 8], ALU.add, [B("PL2")], [B("PL3")])
                    srcs = [(PL[2], "PL2"), (PL[3], "PL3")]
                for hf in range(2):
                    pr = slice(hf * 64, hf * 64 + 64)
                    src, sn = srcs[hf]
                    if (not is_sample) and blk == 0:
                        tt(T1[pr, 0:N], src[pr, 15:L], INVC0[pr, c, 0:N], ALU.mult, [B(sn), B("INVC0")], [B("T1")])
                        tt(MB[pr, c, 0:N], T1[pr, 0:N], UB[pr, c, 15:L], ALU.subtract, [B("T1"), B("UB")], [B("MB")])
                    else:
                        stt(MB[pr, c, 0:N], src[pr, 15:L], VEC[pr, vo + 108 + c:vo + 109 + c], UB[pr, c, 15:L],
                            ALU.mult, ALU.subtract, [B(sn), B("VEC"), B("UB")], [B("MB")])
                ps, pb = rD.next()
                mm(ps[:, 0:N], WPOOL[:, c, :], MB[:, c, 0:N], True, True, [B("WPOOL"), B("MB")], [pb])
                actf(MIXT[:, 3 + c, 0:N], ps[:, 0:N], AF.Identity, [pb, B("VEC")], [B("MIXT")], scale=VEC[:, vo + 16 + c:vo + 17 + c])
            if is_last:
                out_events.append(POOL.dma(o_pool[s].ap()[l], UB[:, :, N:N + 15], [B("UB")], []))
            cpy(UB[:, :, 0:15], UB[:, :, N:N + 15], [B("UB")], [B("UB")])
            ck(7)
            gsub = ROWS[:, l * 192 + 128:l * 192 + 192]
            lam_init = 0.8 - 0.6 * math.exp(-0.3 * l)
            pending_tail = [None]
            def head_post(kind, h):
                for ti, rw in enumerate(rowsl):
                    if kind == "a":
                        DVE.op(lambda e, o=RR[0:rw, 0:1], i=OM[0][0:rw, ti, 64:65]: e.reciprocal(out=o, in_=i), [B("OM0")], [B("RR")])
                        yield
                        DVE.op(lambda e, o=RR[0:rw, 1:2], i=OM[1][0:rw, ti, 64:65]: e.reciprocal(out=o, in_=i), [B("OM1")], [B("RR")])
                        yield
                        tt(RR[0:rw, 1:2], RR[0:rw, 1:2], LAM[0:rw, 2 + l:3 + l], ALU.mult, [B("RR"), B("LAM")], [B("RR")])
                        yield
                        ts(DD[0:rw, 0, :], OM[0][0:rw, ti, 0:64], RR[0:rw, 0:1], None, ALU.mult, None, [B("OM0"), B("RR")], [B("DD")])
                        yield
                        stt(DD[0:rw, 1, :], OM[1][0:rw, ti, 0:64], RR[0:rw, 1:2], DD[0:rw, 0, :], ALU.mult, ALU.add,
                            [B("OM1"), B("RR"), B("DD")], [B("DD")])
                        yield
                        tt(JUNK[0:rw, 0:64], DD[0:rw, 1, :], DD[0:rw, 1, :], ALU.mult, [B("DD")], [B("JUNK")])
                        yield
                        DVE.op(lambda e, o=SSD[0:rw, 0:1], i=JUNK[0:rw, 0:64]: e.reduce_sum(out=o, in_=i, axis=AX.X), [B("JUNK")], [B("SSD")])
                        yield
                        actf(SSD[0:rw, 1:2], SSD[0:rw, 0:1], AF.Ln, [B("SSD")], [B("SSD")], bias=EPS, scale=1.0 / 64.0)
                        yield
                        actf(SSD[0:rw, 1:2], SSD[0:rw, 1:2], AF.Exp, [B("SSD")], [B("SSD")], scale=-0.5)
                        yield
                        ts(SSD[0:rw, 1:2], SSD[0:rw, 1:2], 1.0 - lam_init, None, ALU.mult, None, [B("SSD")], [B("SSD")])
                        yield
                        stt(MIXTOK[0:rw, ti, h * 64:h * 64 + 64], DD[0:rw, 1, :], SSD[0:rw, 1:2], gsub[0:rw, :], ALU.mult, ALU.mult,
                            [B("DD"), B("SSD"), B("ROWS")], [B("MIXTOK")])
                        yield
                    else:
                        DVE.op(lambda e, o=RR[0:rw, 2:3], i=OM[2][0:rw, ti, 64:65]: e.reciprocal(out=o, in_=i), [B("OM2")], [B("RR")])
                        yield
                        ts(MIXTOK[0:rw, ti, 384 + h * 64:384 + h * 64 + 64], OM[2][0:rw, ti, 0:64], RR[0:rw, 2:3], None, ALU.mult, None,
                           [B("OM2"), B("RR")], [B("MIXTOK")])
                        yield
            bg = []
            def bg_step():
                while bg:
                    try:
                        next(bg[0])
                        return
                    except StopIteration:
                        bg.pop(0)
            def bg_flush():
                while bg:
                    bg_step()

            for kind in ("a", "c"):
                Ksc, KSTG, KSN = (KdT, KST, "KST") if kind == "a" else (KcT, KCST, "KCST")
                Vsc, VSTG, VSN = (Vds, VST, "VST") if kind == "a" else (Vcs, VCST, "VCST")
                KscN = "KdT" if kind == "a" else "KcT"
                VscN = "Vds" if kind == "a" else "Vcs"
                kb = kbb = None
                for h in range(6):
                    if kind == "a" and h % 2 == 0:
                        kb, kbb = rKB.next()
                        if t0 > 0:
                            SY.dma(kb[:, 0:t0], Ksc.ap()[:, h // 2, 0:t0], [B(KscN)], [kbb])
                        cpy(kb[:, t0:t0 + N], KSTG[:, h // 2, 0:N], [B(KSN)], [kbb])
                    elif kind == "c":
                        kb, kbb = rKB.next()
                        if t0 > 0:
                            SY.dma(kb[0:96, 0:t0], Ksc.ap()[:, h, 0:t0], [B(KscN)], [kbb])
                        cpy(kb[0:96, t0:t0 + N], KSTG[:, h, 0:N], [B(KSN)], [kbb])
                    vb, vbb = rVB.next()
                    if kt0 > 0:
                        SY.dma(vb[:, 0:kt0, :], Vsc.ap()[:, h, 0:kt0, :], [B(VscN)], [vbb])
                    cpy(vb[:, kt0:kt0 + TTq, :], VSTG[:, h, 0:TTq, :], [B(VSN)], [vbb])
                    g = attn_pass(kind, l, h, N, t0, kt0, nkt, last_ksz, is_sample, kb, kbb, vb, vbb, bg_step, bg_flush)
                    next(g)
                    if pending_tail[0] is not None:
                        pending_tail[0]()
                    this_tail = next(g)
                    def full_tail(this_tail=this_tail, kind=kind, h=h):
                        this_tail()
                        bg.append(head_post(kind, h))
                    pending_tail[0] = full_tail
            if pending_tail[0] is not None:
                pending_tail[0]()
                pending_tail[0] = None
            bg_flush()
            for ch in range(6):
                ps, pb = rD.next()
                for ti, rw in enumerate(rowsl):
                    mm(ps[:, ti * 128:ti * 128 + rw], MIXTOK[0:rw, ti, ch * 128:(ch + 1) * 128], ident_bf[0:rw, 0:rw], True, True,
                       [B("MIXTOK"), B("ident_bf")], [pb])
                mc = ch if ch < 3 else ch + 2
                actf(MIXT[:, mc, 0:N], ps[:, 0:N], AF.Copy, [pb], [B("MIXT")])
            ck(9)
            for mc in range(8):
                wf, wfb = rWF.next()
                SY.dma(wf[:], wsc["w_out"].ap()[l, mc * 128:(mc + 1) * 128, :].rearrange("p (a b) -> p a b", a=8),
                       [B("ws_w_out_%d_%d" % (l, mc))], [wfb])
                ps, pb = rD.next()
                for kc in range(8):
                    mm(ps[:, 0:N], wf[:, kc, :], MIXT[:, kc, 0:N], kc == 0, kc == 7, [wfb, B("MIXT")], [pb])
                stt(XT[:, mc, 0:N], ps[:, 0:N], PRM[:, l, s, 2, mc:mc + 1], XT[:, mc, 0:N], ALU.mult, ALU.add,
                    [pb, B("PRM"), B(XTN)], [B(XTN)])
            ck(10)
            norm(N, P_(3), P_(4))
            for dps, dpb in DACC:
                for bk in range(2):
                    mm(dps[:, bk * 512:(bk + 1) * 512], ZR[:, 0:128], ZR[:, 0:512], True, False, [B("ZR")], [dpb])
            def down_group(grp, wd, wdb, ab, abb):
                for mc in range(8):
                    dps, dpb = DACC[mc // 4]
                    for f in range(2):
                        mm(dps[:, (mc % 4) * NB:(mc % 4) * NB + N], wd[:, f, mc * 128:(mc + 1) * 128], ab[:, f, 0:N],
                           False, grp == 10 and f == 1, [wdb, abb], [dpb])
            prev_down = None
            for grp in range(11):
                wu, wub = rWU.next(); wd, wdb = rWD.next()
                SY.dma(wu[:], wsc["w_up"].ap()[l, grp * 128:(grp + 1) * 128, :].rearrange("p (a b c d) -> p a b c d", a=2, b=2, c=8),
                       [B("ws_w_up_%d_%d" % (l, grp))], [wub])
                SY.dma(wd[:], wsc["w_down"].ap()[l, grp * 128:(grp + 1) * 128, :].rearrange("p (a b) -> p a b", a=2),
                       [B("ws_w_down_%d_%d" % (l, grp))], [wdb])
                ab, abb = rACTB.next()
                for f in range(2):
                    fc = grp * 2 + f
                    gps, gpb = rF.next()
                    for kc in range(8):
                        mm(gps[:, 0:N], wu[:, f, 0, kc, :], HT[:, kc, 0:N], kc == 0, kc == 7, [wub, B("HT")], [gpb])
                    vps, vpb = rF.next()
                    for kc in range(8):
                        mm(vps[:, 0:N], wu[:, f, 1, kc, :], HT[:, kc, 0:N], kc == 0, kc == 7, [wub, B("HT")], [vpb])
                    g, gb = rGSB.next()
                    cw = vo + 20
                    a, ab_ = rAT.next()
                    actf(g[:, 2:2 + N], gps[:, 0:N], AF.Copy, [gpb], [gb])
                    actf(a[:, 0:N], gps[:, 0:N], AF.Identity, [gpb, B("VEC")], [ab_],
                         bias=VEC[:, vo + 86 + fc:vo + 87 + fc], scale=VEC[:, cw + 44 + fc:cw + 45 + fc])
                    cpy(g[:, 0:2], GCAR[:, fc, :], [B("GCAR")], [gb])
                    cpy(GCAR[:, fc, :], g[:, N:N + 2], [gb], [B("GCAR")])
                    stt(a[:, 0:N], g[:, 1:1 + N], VEC[:, cw + 22 + fc:cw + 23 + fc], a[:, 0:N], ALU.mult, ALU.add, [gb, B("VEC"), ab_], [ab_])
                    stt(a[:, 0:N], g[:, 0:N], VEC[:, cw + fc:cw + 1 + fc], a[:, 0:N], ALU.mult, ALU.add, [gb, B("VEC"), ab_], [ab_])
                    st, stb = rST.next()
                    actf(st[:, 0:N], a[:, 0:N], AF.Silu, [ab_], [stb])
                    tt(ab[:, f, 0:N], st[:, 0:N], vps[:, 0:N], ALU.mult, [stb, vpb], [abb])
                if prev_down is not None:
                    down_group(*prev_down)
                prev_down = (grp, wd, wdb, ab, abb)
            down_group(*prev_down)
            for mc in range(8):
                dps, dpb = DACC[mc // 4]
                stt(XT[:, mc, 0:N], dps[:, (mc % 4) * NB:(mc % 4) * NB + N], PRM[:, l, s, 5, mc:mc + 1], XT[:, mc, 0:N], ALU.mult, ALU.add,
                    [dpb, B("PRM"), B(XTN)], [B(XTN)])
            if is_last:
                out_events.append(POOL.dma(o_conv[s].ap()[l], GCAR[:], [B("GCAR")], []))
            ck(11)
            if l == 0:
                POOL.dma(xscr.ap()[:, :, tq:tq + N], XT[:, :, 0:N], [B(XTN)], [B("xscr")])
            else:
                actf(SQ[:, :, 0:N], XT[:, :, 0:N], AF.Square, [B(XTN)], [B("SQ")])
                ps, pb = rD.next()
                for c in range(8):
                    mm(ps[:, 0:N], ones_bf[:, :], SQ[:, c, 0:N], c == 0, c == 7, [B("SQ"), B("ones_bf")], [pb])
                rstd_from(RSTD[:, 0:N], ps[:, 0:N], float(D), [pb], [B("RSTD")])
                for c in range(8):
                    y, yb = rYT.next()
                    stt(y[:, 0:N], XT[:, c, 0:N], VEC[:, 2 * VL + c:2 * VL + c + 1], RSTD[:, 0:N], ALU.mult, ALU.mult,
                        [B(XTN), B("VEC"), B("RSTD")], [yb])
                    out_events.append(POOL.dma(yT[s].ap()[:, c, tq:tq + N], y[:, 0:N], [yb], []))

        class _Stop(Exception):
            pass
        nblocks_emitted = [0]
        _orig_block = block
        def block(*a_, **k_):
            if DEBUG_STOP is not None and nblocks_emitted[0] >= DEBUG_STOP.get("nblocks", 10 ** 9):
                return
            nblocks_emitted[0] += 1
            try:
                _orig_block(*a_, **k_)
            except _CkStop:
                pass
        for s in (DEBUG_STOP.get("seqs", range(3)) if DEBUG_STOP else range(3)):
            is_sample = (s == 2)
            for l in (DEBUG_STOP.get("layers", range(DEPTH)) if DEBUG_STOP else range(DEPTH)):
                load_small_weights(l)
                if is_sample:
                    SY.dma(UB[:, :, 0:15], poolst.ap()[l], [], [B("UB")])
                    SY.dma(GCAR[:], convst.ap()[l], [], [B("GCAR")])
                    for j_ in range(3):
                        POOL.dma(KdT.ap()[:, j_, 0:PAST], cak.ap()[l, :, j_, :], [], [B("KdT")])
                    for h_ in range(6):
                        POOL.dma(Vds.ap()[:, h_, 0:32, :], cav.ap()[l, :, h_, :, :], [], [B("Vds")])
                    for h_ in range(6):
                        POOL.dma(KcT.ap()[64:96, h_, 0:PAST], ckr.ap()[l, 64:96, :], [], [B("KcT")])
                    for pc in range(PAST // NB):
                        POOL.dma(LATT[:, 0:NB], clat.ap()[l, :, pc * NB:(pc + 1) * NB], [], [B("LATT")])
                        mla_kv_from_latt(l, pc * NB, NB, [128, 128], with_kr=False)
                    block(s, l, 0, TS, PAST, True, True)
                else:
                    DVE.op(lambda e: e.memset(UB[:, :, 0:15], 0.0), [], [B("UB")])
                    DVE.op(lambda e: e.memset(GCAR[:], 0.0), [], [B("GCAR")])
                    nblk = T // NB
                    for blk in range(nblk):
                        block(s, l, blk, NB, blk * NB, False, blk == nblk - 1)
        for ev in out_events:
            SY.wait(ev)
        STATS.update({e_.name: (e_.sem.cnt if e_.sem else 0, len(e_.ops)) for e_ in R.engs})
        STATS["dma"] = sum(x.cnt for x in R.dsems)
        with nc.Block() as blockctx:
            R.replay(blockctx)
    return nc


def _prep_weights(inp):
    sw = np.concatenate([np.arange(16, 32), np.arange(16)])
    AQ = np.arange(0, 384); AK = np.arange(384, 768); AV = np.arange(768, 1152); Bc = np.arange(1152, 1408)
    CQ = np.arange(1408, 1664); CKV = np.arange(1664, 1792); CKR = np.arange(1792, 1824)
    def pad128(ix):
        return np.concatenate([ix, np.repeat(ix[:1], 128 - len(ix))]) if len(ix) < 128 else ix
    fchunks = [AQ[0:128], AQ[128:256], AQ[256:384], AK[0:128], AK[128:256], AK[256:384], Bc[0:128], Bc[128:256],
               CQ[0:128], CQ[128:256], CKV, pad128(np.tile(CKR, 3)), pad128(np.tile(CKR[sw], 3))]
    tcols = np.concatenate([AK, CKV, AV, CKR, CKR[sw]])
    out = {k: [] for k in ("w_inF", "w_inT", "w_q", "w_kv", "w_pool", "w_out", "w_up", "w_down")}
    for l in range(DEPTH):
        w_in = np.asarray(inp["w_in"][l])
        out["w_inF"].append(np.stack([_kchunk(w_in[:, ix]) for ix in fchunks]).reshape(13 * 128, 8 * 128))
        out["w_inT"].append(_kchunk(w_in[:, tcols]).reshape(128, 8 * 960))
        wq = np.asarray(inp["w_q_up"][l])
        qcols = [np.arange(576)] + [hh * 96 + 64 + sw for hh in range(6)]
        out["w_q"].append(_kchunk(wq[:, np.concatenate(qcols)]).reshape(128, 2 * 768))
        wkv = np.asarray(inp["w_kv_up"][l])
        kcols = [hh * 128 + np.arange(64) for hh in range(6)] + [hh * 128 + 64 + np.arange(64) for hh in range(6)]
        out["w_kv"].append(np.ascontiguousarray(wkv[:, np.concatenate(kcols)]))
        pw = np.asarray(inp["pool_w"][l])
        wp = np.zeros((128, 2, 128), np.float32)
        for c in range(2):
            wp[0:64, c, 0:64] = pw[2 * c]
            wp[64:128, c, 64:128] = pw[2 * c + 1]
        out["w_pool"].append(wp.reshape(128, 256))
        wo = np.asarray(inp["w_out"][l])
        out["w_out"].append(np.stack([_kchunk(wo[:, mc * 128:(mc + 1) * 128]) for mc in range(8)]).reshape(8 * 128, 8 * 128))
        wu = np.asarray(inp["w_up"][l])
        wuk = _kchunk(wu)
        wug = wuk.reshape(128, 8, 2, 11, 2, 128)
        out["w_up"].append(np.ascontiguousarray(wug.transpose(3, 0, 4, 2, 1, 5)).reshape(11 * 128, 4096))
        wd = np.asarray(inp["w_down"][l])
        wdg = wd.reshape(11, 2, 128, 1024)
        out["w_down"].append(np.ascontiguousarray(wdg.transpose(0, 2, 1, 3)).reshape(11 * 128, 2048))
    return {k: np.stack(v).astype(np.float32) for k, v in out.items()}


def prep_inputs(inp):
    inp = {k: np.asarray(v) for k, v in inp.items()}
    ct = _const_tables()
    W = _prep_weights(inp)
    vecs = np.zeros((128, 2 * VL + 8), np.float32)
    for l in range(DEPTH):
        o = l * VL
        vecs[:, o + 0:o + 8] = _fm(inp["g_mix"][l])
        vecs[:, o + 8:o + 16] = _fm(inp["g_ffn"][l])
        vecs[:, o + 16:o + 18] = _fm(inp["pool_scale"][l])
        vecs[:, o + 18:o + 20] = _fm(inp["c_q_norm_g"][l])
        for j in range(3):
            vecs[:, o + 20 + j * 22:o + 20 + (j + 1) * 22] = _fm(inp["conv_w"][l, j])
        vecs[:, o + 86:o + 108] = _fm(inp["conv_b"][l])
        vecs[:, o + 108:o + 110] = ct["invw"]
        vecs[:, o + 110:o + 111] = _fm(inp["c_kv_norm_g"][l])
    for l in range(DEPTH):
        vecs[:, l * VL + 112:l * VL + 160] = _fm(inp["b_ada"][l])
    vecs[:, 2 * VL:2 * VL + 8] = _fm(inp["g_final"])
    rows = np.zeros((128, DEPTH * 192), np.float32)
    lamv = np.zeros((128, DEPTH * 4 * 32), np.float32)
    for l in range(DEPTH):
        rows[:, l * 192:l * 192 + 128] = inp["c_kv_norm_g"][l][None, :]
        rows[:, l * 192 + 128:l * 192 + 192] = inp["a_subln_g"][l][None, :]
        for i, k in enumerate(("lam_q1", "lam_k1", "lam_q2", "lam_k2")):
            lamv[:, (l * 4 + i) * 32:(l * 4 + i + 1) * 32] = inp[k][l][None, :]
    wada = np.stack([_kchunk(inp["w_ada"][l]) for l in range(DEPTH)])
    common = dict(wada=wada, rows=rows, lamv=lamv, relb=np.ascontiguousarray(inp["rel_bias"]),
                  r15c=np.ascontiguousarray(inp["rel_bias"][15][:, None]),
                  r15r=np.ascontiguousarray(np.broadcast_to(inp["rel_bias"][15][None, :], (128, 6))),
                  ohv=ct["ohv"], ropeF=ct["ropeF"], ropeT=ct["ropeT"], invcnt0=ct["invcnt0"], ident=ct["ident"], jmat=ct["jmat"])
    common.update(W)
    in_maps = []
    for c in range(8):
        m = dict(common)
        m["vecs"] = vecs
        for i, b in enumerate((2 * c, 2 * c + 1)):
            m["xT%d" % i] = np.ascontiguousarray(inp["x_prompt"][b].reshape(T, 8, 128).transpose(2, 1, 0))
        m["xT2"] = np.ascontiguousarray(inp["x_sample"][c].reshape(TS, 8, 128).transpose(2, 1, 0))
        cs = np.stack([inp["c_prompt"][2 * c], inp["c_prompt"][2 * c + 1], inp["c_sample"][c]])
        m["cT"] = np.ascontiguousarray(cs.reshape(3, 8, 128).transpose(2, 1, 0))
        m["poolst"] = np.ascontiguousarray(inp["state_b_pool"][:, c].reshape(DEPTH, 15, 2, 128).transpose(0, 3, 2, 1))
        m["convst"] = np.ascontiguousarray(inp["state_ffn_conv"][:, c].reshape(DEPTH, 2, NFC, 128).transpose(0, 3, 2, 1))
        ck = inp["cache_a_k"][:, c].reshape(DEPTH, PAST, 3, 128)
        m["cak"] = np.ascontiguousarray(ck.transpose(0, 3, 2, 1))
        cv = inp["cache_a_v"][:, c].reshape(DEPTH, 32, 128, 6, 64)
        cvo = np.ones((DEPTH, 128, 6, 32, 65), np.float32)
        cvo[..., 0:64] = cv.transpose(0, 2, 3, 1, 4)
        m["cav"] = cvo
        m["clat"] = np.ascontiguousarray(inp["cache_c_latent"][:, c].transpose(0, 2, 1))
        kr = inp["cache_c_krope"][:, c].transpose(0, 2, 1)
        m["ckr"] = np.ascontiguousarray(np.concatenate([kr, kr, kr], axis=1))
        in_maps.append(m)
    return in_maps


def kernel(**inp):
    in_maps = prep_inputs(inp)
    nc = build_program()
    res = run_bass_kernel_spmd(nc, in_maps, core_ids=list(range(8)))
    return assemble(res.results)


def assemble(rs):
    def unT(a):
        return np.ascontiguousarray(a.transpose(2, 1, 0).reshape(a.shape[2], D))
    y_p = np.stack([unT(rs[b // 2]["yT%d" % (b % 2)]) for b in range(16)])
    y_s = np.stack([unT(rs[c]["yT2"]) for c in range(8)])
    def gp(name):
        return np.stack([rs[b // 2]["%s%d" % (name, b % 2)] for b in range(16)], axis=1)
    def gs(name):
        return np.stack([rs[c]["%s2" % name] for c in range(8)], axis=1)
    def poolfix(a):
        return np.ascontiguousarray(a.transpose(0, 1, 4, 3, 2).reshape(a.shape[0], a.shape[1], 15, 256))
    def convfix(a):
        return np.ascontiguousarray(a.transpose(0, 1, 4, 3, 2).reshape(a.shape[0], a.shape[1], 2, FFN))
    outs = (y_p, y_s,
            gp("ak").reshape(DEPTH, 16, T, 6, 64), gp("av").reshape(DEPTH, 16, T, 6, 64), gp("lat"), gp("kr"),
            poolfix(gp("pool")), convfix(gp("conv")),
            gs("ak").reshape(DEPTH, 8, TS, 6, 64), gs("av").reshape(DEPTH, 8, TS, 6, 64), gs("lat"), gs("kr"),
            poolfix(gs("pool")), convfix(gs("conv")))
    return tuple(np.ascontiguousarray(o, dtype=np.float32) for o in outs)
```

```python
import math
from contextlib import ExitStack
import numpy as np
import jax
import jax.numpy as jnp
import concourse.bass as bass
import concourse.mybir as mybir
from concourse.bass_utils import run_bass_kernel_spmd

F32 = mybir.dt.float32
BF16 = mybir.dt.bfloat16
AF = mybir.ActivationFunctionType
ALU = mybir.AluOpType
AX = mybir.AxisListType

D = 1024; T = 4096; DEPTH = 2; PAST = 4096; TS = 64
FFN = 2816; NFC = 22
EPS = 1e-6
NB = 256
TKV = PAST + TS
NKT = 33
VL = 160
SC_A = 32 ** -0.5
SC_C = 96 ** -0.5
SEM_LIMIT = 48000

class Sem:
    def __init__(self, h, idx):
        self.h = h; self.idx = idx; self.cnt = 0

class Ev:
    __slots__ = ("sem", "val", "clock")
    def __init__(self, sem, val, clock):
        self.sem = sem; self.val = val; self.clock = clock

class Buf:
    __slots__ = ("name", "w", "r", "excl")
    def __init__(self, name):
        self.name = name; self.w = None; self.r = {}
        self.excl = name.startswith("ps")

class Eng:
    def __init__(self, rec, name, sem, self_sync):
        self.rec = rec; self.name = name; self.sem = sem
        self.seen = {}
        self.ops = []
        self.self_sync = self_sync

    def wait(self, ev):
        if ev is None:
            return
        if self.sem is not None and ev.sem is self.sem and not self.self_sync:
            return
        if self.seen.get(ev.sem.idx, 0) >= ev.val:
            return
        h, v = ev.sem.h, ev.val
        self.ops.append(("wait", h, v))
        for k, c in ev.clock.items():
            if self.seen.get(k, 0) < c:
                self.seen[k] = c
        self.seen[ev.sem.idx] = v

    def _deps(self, reads, writes):
        for b in reads:
            self.wait(b.w)
            if b.excl:
                for e in list(b.r.values()):
                    if e.sem is not self.sem:
                        self.wait(e)
        for b in writes:
            self.wait(b.w)
            for e in list(b.r.values()):
                self.wait(e)

    def _reg(self, ev, reads, writes):
        for b in reads:
            b.r[ev.sem.idx] = ev
        for b in writes:
            b.w = ev; b.r = {}

    def op(self, fn, reads=(), writes=()):
        if self.sem.cnt >= SEM_LIMIT:
            old = self.sem
            self.ops.append(("wait", old.h, old.cnt))
            self.seen[old.idx] = old.cnt
            self.sem = self.rec.new_sem(self.name)
        self._deps(reads, writes)
        self.sem.cnt += 1
        self.ops.append(("op", fn, self.sem.h, 1))
        ev = Ev(self.sem, self.sem.cnt, dict(self.seen))
        self._reg(ev, reads, writes)
        return ev

    def dma(self, out, in_, reads=(), writes=(), **kw):
        self._deps(reads, writes)
        s = self.rec.next_dma_sem(self.name)
        if s.cnt > 0:
            self.wait(Ev(s, s.cnt * 16, {}))
        s.cnt += 1
        self.ops.append(("op", (lambda e, o=out, i=in_, k=kw: e.dma_start(out=o, in_=i, **k)), s.h, 16))
        ev = Ev(s, s.cnt * 16, dict(self.seen))
        self._reg(ev, reads, writes)
        return ev

class Rec:
    def __init__(self, nc, es, n_dma_sems=80):
        self.nc = nc
        self.es = es
        self.nsem = 0
        def mk(name, idx=None):
            return self.new_sem(name)
        self.pe = Eng(self, "pe", mk("s_pe", 0), False)
        self.act = Eng(self, "act", mk("s_act", 1), True)
        self.dve = Eng(self, "dve", mk("s_dve", 2), True)
        self.pool = Eng(self, "pool", mk("s_pool", 3), True)
        self.sync = Eng(self, "sync", None, False)
        self.dsems = [mk("s_d%d" % i, 4 + i) for i in range(n_dma_sems)]
        self.dpool = {"pool": self.dsems[0:28], "sync": self.dsems[28:]}
        self.dpi = {"pool": 0, "sync": 0}
        self.engs = [self.pe, self.act, self.dve, self.pool, self.sync]

    def new_sem(self, name):
        k = self.nsem; self.nsem += 1
        return Sem(self.es.enter_context(self.nc.semaphore("%s_%d" % (name, k))), k)

    def next_dma_sem(self, qname):
        lst = self.dpool[qname]
        s = lst[self.dpi[qname]]
        self.dpi[qname] = (self.dpi[qname] + 1) % len(lst)
        return s

    def barrier(self):
        for e in self.engs:
            for f in (self.pe, self.act, self.dve, self.pool):
                if f is not e and f.sem.cnt > 0:
                    e.wait(Ev(f.sem, f.sem.cnt, {}))
            for s in self.dsems:
                if s.cnt > 0:
                    e.wait(Ev(s, s.cnt * 16, {}))

    def replay(self, block):
        def run(eng):
            def f(h):
                for o in eng.ops:
                    if o[0] == "wait":
                        h.wait_ge(o[1], o[2])
                    else:
                        o[1](h).then_inc(o[2], o[3])
            return f
        block.tensor(run(self.pe))
        block.scalar(run(self.act))
        block.vector(run(self.dve))
        block.gpsimd(run(self.pool))
        block.sync(run(self.sync))


def _t5_bucket(rel):
    half = 16; exact = 8
    ret = jnp.where(rel > 0, half, 0)
    n = jnp.abs(rel)
    nf = jnp.maximum(n, 1).astype(jnp.float32)
    large = exact + (jnp.log(nf / exact) / math.log(128 / exact) * (half - exact)).astype(jnp.int32)
    large = jnp.minimum(large, half - 1)
    return ret + jnp.where(n < exact, n, large)

def _const_tables():
    with jax.default_device(jax.devices("cpu")[0]):
        delta = jnp.arange(383, dtype=jnp.int32) - 255
        bk = np.asarray(_t5_bucket(delta))
        ohv = np.zeros((32, 384), np.float32)
        ohv[bk, np.arange(383)] = 1.0
        half = 16
        inv = 1.0 / (10000.0 ** (jnp.arange(half, dtype=jnp.float32) / half))
        pos = jnp.arange(TKV, dtype=jnp.int32)
        ang = pos.astype(jnp.float32)[:, None] * inv[None, :]
        cos = np.asarray(jnp.cos(ang)); sin = np.asarray(jnp.sin(ang))
    cc = np.concatenate([cos, cos], 1)
    ss = np.concatenate([-sin, sin], 1)
    ropeF = np.zeros((96, 2, TKV), np.float32)
    for r in range(3):
        ropeF[r * 32:(r + 1) * 32, 0] = cc.T
        ropeF[r * 32:(r + 1) * 32, 1] = ss.T
    ropeT = np.zeros((128, NKT, 2, 32), np.float32)
    for kt in range(NKT):
        n = min(128, TKV - kt * 128)
        ropeT[:n, kt, 0] = cc[kt * 128:kt * 128 + n]
        ropeT[:n, kt, 1] = ss[kt * 128:kt * 128 + n]
    wins = (2, 4, 8, 16)
    invcnt0 = np.zeros((128, 2, NB), np.float32)
    invw = np.zeros((128, 2), np.float32)
    for c in range(2):
        for p in range(128):
            w = wins[2 * c + p // 64]
            invw[p, c] = np.float32(1.0) / np.float32(w)
            invcnt0[p, c] = np.float32(1.0) / np.minimum(np.arange(NB) + 1, w).astype(np.float32)
    ident = np.eye(128, dtype=np.float32)
    jmat = np.ascontiguousarray(ident[::-1])
    return dict(ohv=ohv, ropeF=ropeF, ropeT=ropeT, invcnt0=invcnt0, invw=invw, ident=ident, jmat=jmat)


def _kchunk(w):
    K, n = w.shape
    return np.ascontiguousarray(w.reshape(K // 128, 128, n).transpose(1, 0, 2))

def _fm(v):
    return np.ascontiguousarray(v.reshape(-1, 128).T)


SEQ_T = [T, T, TS]
STATS = {}
DEBUG_STOP = None

def build_program():
    nc = bass.Bass("TRN2", target_bir_lowering=False)
    def din(name, shape, dt=F32):
        return nc.dram_tensor(name, list(shape), dt, kind="ExternalInput")
    def dout(name, shape):
        return nc.dram_tensor(name, list(shape), F32, kind="ExternalOutput")
    def dscr(name, shape, dt):
        return nc.dram_tensor(name, list(shape), dt, kind="Internal")

    xT = [din("xT0", [128, 8, T]), din("xT1", [128, 8, T]), din("xT2", [128, 8, TS])]
    cT = din("cT", [128, 8, 3])
    wada = din("wada", [DEPTH, 128, 8, 6 * D])
    vecs = din("vecs", [128, 2 * VL + 8])
    rows = din("rows", [128, DEPTH * 192])
    lamv = din("lamv", [128, DEPTH * 4 * 32])
    relb = din("relb", [32, 6])
    r15c = din("r15c", [6, 1])
    r15r = din("r15r", [128, 6])
    ohv = din("ohv", [32, 384])
    ropeF = din("ropeF", [96, 2, TKV])
    ropeT = din("ropeT", [128, NKT, 2, 32])
    invcnt0 = din("invcnt0", [128, 2, NB])
    ident_in = din("ident", [128, 128])
    jmat_in = din("jmat", [128, 128])
    poolst = din("poolst", [DEPTH, 128, 2, 15])
    convst = din("convst", [DEPTH, 128, NFC, 2])
    cak = din("cak", [DEPTH, 128, 3, PAST])
    cav = din("cav", [DEPTH, 128, 6, 32, 65])
    clat = din("clat", [DEPTH, 128, PAST])
    ckr = din("ckr", [DEPTH, 96, PAST])
    WSH = dict(w_inF=[13 * 128, 8 * 128], w_inT=[128, 8 * 960], w_q=[128, 2 * 768], w_kv=[128, 768],
               w_pool=[128, 256], w_out=[8 * 128, 8 * 128], w_up=[11 * 128, 4096], w_down=[11 * 128, 2048])
    win = {k: din(k, [DEPTH] + v) for k, v in WSH.items()}
    wsc = {k: dscr(k + "_bf", [DEPTH] + v, BF16) for k, v in WSH.items()}

    yT = [dout("yT0", [128, 8, T]), dout("yT1", [128, 8, T]), dout("yT2", [128, 8, TS])]
    o_ak = [dout("ak%d" % s, [DEPTH, SEQ_T[s], 384]) for s in range(3)]
    o_av = [dout("av%d" % s, [DEPTH, SEQ_T[s], 384]) for s in range(3)]
    o_lat = [dout("lat%d" % s, [DEPTH, SEQ_T[s], 128]) for s in range(3)]
    o_kr = [dout("kr%d" % s, [DEPTH, SEQ_T[s], 32]) for s in range(3)]
    o_pool = [dout("pool%d" % s, [DEPTH, 128, 2, 15]) for s in range(3)]
    o_conv = [dout("conv%d" % s, [DEPTH, 128, NFC, 2]) for s in range(3)]

    xscr = dscr("xscr", [128, 8, T], F32)
    fscr = dscr("fscr", [6, 384], F32)
    KdT = dscr("KdT", [128, 3, TKV], BF16)
    KcT = dscr("KcT", [96, 6, TKV], BF16)
    KRs = dscr("KRs", [96, TKV], BF16)
    Vds = dscr("Vds", [128, 6, NKT, 65], BF16)
    Vcs = dscr("Vcs", [128, 6, NKT, 65], BF16)

    with ExitStack() as es:
        R = Rec(nc, es)
        PE, ACT, DVE, POOL, SY = R.pe, R.act, R.dve, R.pool, R.sync
        def sb(name, shape, dt=F32):
            nb = int(np.prod(shape[1:])) * (2 if dt == BF16 else 4)
            STATS.setdefault("sbuf", {})[name] = nb
            return es.enter_context(nc.sbuf_tensor(name, list(shape), dt))
        bufs = {}
        def B(name):
            if name not in bufs:
                bufs[name] = Buf(name)
            return bufs[name]

        def mm(out, lhsT, rhs, start, stop, reads, writes, tp=None):
            if tp is None:
                PE.op(lambda e: e.matmul(out, lhsT, rhs, start=start, stop=stop), reads, writes)
            else:
                PE.op(lambda e: e.matmul(out, lhsT, rhs, start=start, stop=stop, tile_position=tp), reads, writes)
        def actf(out, in_, func, reads, writes, bias=0.0, scale=1.0, accum=None):
            if accum is None:
                ACT.op(lambda e: e.activation(out=out, in_=in_, func=func, bias=bias, scale=scale), reads, writes)
            else:
                ACT.op(lambda e: e.activation(out=out, in_=in_, func=func, bias=bias, scale=scale, accum_out=accum), reads, writes)
        def tt(out, a, b, op, reads, writes, eng=None):
            (eng or DVE).op(lambda e: e.tensor_tensor(out=out, in0=a, in1=b, op=op), reads, writes)
        def ts(out, a, s1, s2, op0, op1, reads, writes, eng=None):
            if s2 is None:
                (eng or DVE).op(lambda e: e.tensor_single_scalar(out=out, in_=a, scalar=s1, op=op0), reads, writes)
            else:
                (eng or DVE).op(lambda e: e.tensor_scalar(out=out, in0=a, scalar1=s1, scalar2=s2, op0=op0, op1=op1), reads, writes)
        def stt(out, a, s, b, op0, op1, reads, writes, eng=None):
            (eng or DVE).op(lambda e: e.scalar_tensor_tensor(out=out, in0=a, scalar=s, in1=b, op0=op0, op1=op1), reads, writes)
        def cpy(out, in_, reads, writes, eng=None):
            (eng or DVE).op(lambda e: e.tensor_copy(out=out, in_=in_), reads, writes)
        def rstd_from(out, ss_ap, n, reads, writes):
            actf(out, ss_ap, AF.Ln, reads, writes, bias=EPS, scale=1.0 / n)
            actf(out, out, AF.Exp, writes, writes, scale=-0.5)

        PS = [es.enter_context(nc.psum_tensor("ps%d" % i, [128, 512], F32)) for i in range(4)]
        SS = [es.enter_context(nc.psum_tensor("pss%d" % i, [128, 1024], F32)) for i in range(2)]
        class Rot:
            def __init__(self, items, names):
                self.items = items; self.names = names; self.i = 0
            def next(self):
                k = self.i; self.i = (self.i + 1) % len(self.items)
                return self.items[k], B(self.names[k])
        rD = Rot(PS[0:2], ["psD0", "psD1"])
        rS = Rot([SS[0][:, 0:512], SS[0][:, 512:1024], SS[1][:, 0:512], SS[1][:, 512:1024]],
                 ["psS0_0", "psS0_1", "psS1_0", "psS1_1"])
        rA = Rot(PS[2:4], ["psA0", "psA1"])
        DACC = [(SS[0], [B("psS0_0"), B("psS0_1")]), (SS[1], [B("psS1_0"), B("psS1_1")])]
        rF = Rot(PS[0:4], ["psD0", "psD1", "psA0", "psA1"])

        ident_bf = sb("ident_bf", [128, 128], BF16)
        ident_f = sb("ident_f", [128, 128])
        ones_bf = sb("ones_bf", [128, 128], BF16)
        BIAST = sb("BIAST", [128, 2, 6, 128], BF16)
        VEC = sb("VEC", [128, 2 * VL + 8])
        ROWS = sb("ROWS", [128, DEPTH * 192])
        R15 = sb("R15", [128, 6])
        LAM = sb("LAM", [128, 4])
        PRM = sb("PRM", [128, DEPTH, 3, 6, 8])
        INVC0 = sb("INVC0", [128, 2, NB])

        SY.dma(ident_f[:], ident_in.ap(), [], [B("ident_f")])
        POOL.dma(ident_bf[:], ident_in.ap(), [], [B("ident_bf")])
        SY.dma(VEC[:], vecs.ap(), [], [B("VEC")])
        SY.dma(ROWS[:], rows.ap(), [], [B("ROWS")])
        SY.dma(R15[:], r15r.ap(), [], [B("R15")])
        SY.dma(INVC0[:], invcnt0.ap(), [], [B("INVC0")])
        DVE.op(lambda e: e.memset(ones_bf[:], 1.0), [], [B("ones_bf")])
        ZR = sb("ZR", [128, 512], BF16)
        DVE.op(lambda e: e.memset(ZR[:], 0.0), [], [B("ZR")])

        for l in range(DEPTH):
            for k in WSH:
                nr = WSH[k][0]
                for r0 in range(0, nr, 128):
                    POOL.dma(wsc[k].ap()[l, r0:r0 + 128, :], win[k].ap()[l, r0:r0 + 128, :], [], [B("ws_%s_%d_%d" % (k, l, r0 // 128))])

        with ExitStack() as es2:
            def sb2(name, shape, dt=F32):
                return es2.enter_context(nc.sbuf_tensor(name, list(shape), dt))
            WA = [sb2("WA0", [128, 8, 512]), sb2("WA1", [128, 8, 512])]
            CS = sb2("CS", [128, 8, 3]); SCF = sb2("SCF", [128, 8, 3])
            MODT = sb2("MODT", [128, DEPTH, 48, 3])
            LV = sb2("LV", [128, DEPTH * 4 * 32]); LT = sb2("LT", [128, 32]); LS = sb2("LS", [128, 4])
            RELB = sb2("RELB", [32, 6]); OHV = sb2("OHV", [32, 384]); R15C = sb2("R15C", [6, 1])
            FSB = sb2("FSB", [6, 384]); HSB = sb2("HSB", [128, 2, 6, 128]); JM = sb2("JM", [128, 128])
            TMPP = sb2("TMPP", [128, 8])

            SY.dma(CS[:], cT.ap(), [], [B("CS")])
            SY.dma(LV[:], lamv.ap(), [], [B("LV")])
            SY.dma(RELB[:], relb.ap(), [], [B("RELB")])
            SY.dma(OHV[:], ohv.ap(), [], [B("OHV")])
            SY.dma(R15C[:], r15c.ap(), [], [B("R15C")])
            SY.dma(JM[:], jmat_in.ap(), [], [B("JM")])
            actf(SCF[:], CS[:], AF.Silu, [B("CS")], [B("SCF")])
            for l in range(DEPTH):
                mps, mpb = rD.next()
                for pc in range(12):
                    wa = WA[pc % 2]; wab = B("WA%d" % (pc % 2))
                    SY.dma(wa[:], wada.ap()[l, :, :, pc * 512:(pc + 1) * 512], [], [wab])
                    for j in range(4):
                        ch = pc * 4 + j
                        for kc in range(8):
                            mm(mps[:, ch * 3:ch * 3 + 3], wa[:, kc, j * 128:(j + 1) * 128], SCF[:, kc, :],
                               kc == 0, kc == 7, [wab, B("SCF")], [mpb])
                boff = l * VL + 112
                for s in range(3):
                    tt(MODT[:, l, :, s], mps[:, 0:144].rearrange("p (c s) -> p c s", s=3)[:, :, s],
                       VEC[:, boff:boff + 48], ALU.add, [mpb, B("VEC")], [B("MODT")])
                go = l * VL
                for s in range(3):
                    for (dst, scj, gof) in ((0, 8, 0), (3, 32, 8)):
                        ts(TMPP[:], MODT[:, l, scj:scj + 8, s], 1.0, None, ALU.add, None, [B("MODT")], [B("TMPP")])
                        tt(PRM[:, l, s, dst, :], TMPP[:], VEC[:, go + gof:go + gof + 8], ALU.mult,
                           [B("TMPP"), B("VEC")], [B("PRM")])
                    for (dst, j0) in ((1, 0), (2, 16), (4, 24), (5, 40)):
                        cpy(PRM[:, l, s, dst, :], MODT[:, l, j0:j0 + 8, s], [B("MODT")], [B("PRM")])
            for l in range(DEPTH):
                for i in range(2):
                    o = (l * 4 + 2 * i) * 32
                    tt(LT[:], LV[:, o:o + 32], LV[:, o + 32:o + 64], ALU.mult, [B("LV")], [B("LT")])
                    DVE.op(lambda e, a=LS[:, l * 2 + i:l * 2 + i + 1]: e.reduce_sum(out=a, in_=LT[:], axis=AX.X),
                           [B("LT")], [B("LS")])
            actf(LS[:], LS[:], AF.Exp, [B("LS")], [B("LS")])
            for l in range(DEPTH):
                lam_init = 0.8 - 0.6 * math.exp(-0.3 * l)
                tt(LAM[:, l:l + 1], LS[:, 2 * l:2 * l + 1], LS[:, 2 * l + 1:2 * l + 2], ALU.subtract, [B("LS")], [B("LAM")])
                ts(LAM[:, l:l + 1], LAM[:, l:l + 1], lam_init, None, ALU.add, None, [B("LAM")], [B("LAM")])
                ts(LAM[:, 2 + l:3 + l], LAM[:, l:l + 1], -1.0, None, ALU.mult, None, [B("LAM")], [B("LAM")])
            fps, fpb = rD.next()
            mm(fps[0:6, 0:384], RELB[:, :], OHV[:, :], True, True, [B("RELB"), B("OHV")], [fpb])
            ts(FSB[:], fps[0:6, 0:384], R15C[:, 0:1], 1.0 / SC_A, ALU.subtract, ALU.mult, [fpb, B("R15C")], [B("FSB")])
            SY.dma(fscr.ap(), FSB[:], [B("FSB")], [B("fscr")])
            for h in range(6):
                for d in range(2):
                    src = bass.AP(tensor=fscr, offset=h * 384 + (128 if d == 0 else 0), ap=[[1, 128], [1, 128]])
                    SY.dma(HSB[:, d, h, :], src, [B("fscr")], [B("HSB")])
            for d in range(2):
                for h in range(6):
                    bp, bpb = rD.next()
                    mm(bp[:, 0:128], HSB[:, d, h, :], JM[:, :], True, True, [B("HSB"), B("JM")], [bpb])
                    actf(BIAST[:, d, h, :], bp[:, 0:128], AF.Copy, [bpb], [B("BIAST")])
            R.barrier()
            with nc.Block() as blockctx0:
                R.replay(blockctx0)
            for e_ in R.engs:
                e_.ops = []
        XTS = [sb("XTa", [128, 8, NB]), sb("XTb", [128, 8, NB])]
        xsel = [0]
        HT = sb("HT", [128, 8, NB], BF16); SQ = sb("SQ", [128, 8, NB], BF16)
        MIXT = sb("MIXT", [128, 8, NB], BF16)
        TMP = [sb("TMP%d" % i, [128, NB]) for i in range(2)]
        rTMP = Rot(TMP, ["TMP0", "TMP1"])
        RSTD = sb("RSTD", [128, NB]); RSTDQ = sb("RSTDQ", [128, NB]); RSTDKV = sb("RSTDKV", [128, NB])
        WF = [sb("WF%d" % i, [128, 8, 128], BF16) for i in range(4)]
        rWF = Rot(WF, ["WF%d" % i for i in range(4)])
        WT = sb("WT", [128, 8, 960], BF16)
        WQ = sb("WQ", [128, 2, 768], BF16); WKV = sb("WKV", [128, 768], BF16); WPOOL = sb("WPOOL", [128, 2, 128], BF16)
        QDM = sb("QDM", [128, 3, 4, NB], BF16); KST = sb("KST", [128, 3, NB], BF16)
        UB = sb("UB", [128, 2, 15 + NB])
        CQG = sb("CQG", [128, 2, NB], BF16); CQSQ = sb("CQSQ", [128, 2, NB], BF16)
        CKVSQ = sb("CKVSQ", [128, NB], BF16); CKVG = sb("CKVG", [128, NB])
        T1 = sb("T1", [128, NB]); T2 = sb("T2", [128, NB])
        KRST = sb("KRST", [96, NB], BF16); LATT = sb("LATT", [128, NB], BF16)
        OSB = [sb("OSB%d" % i, [128, 928]) for i in range(2)]
        rOSB = Rot(OSB, ["OSB0", "OSB1"])
        JUNK = sb("JUNK", [128, 128]); SSL = sb("SSL", [128, 2]); KT1 = sb("KT1", [128, 32]); KT2 = sb("KT2", [128, 32])
        VST = sb("VST", [128, 6, 2, 65], BF16); VCST = sb("VCST", [128, 6, 2, 65], BF16)
        KCST = sb("KCST", [96, 6, NB], BF16)
        QC = sb("QC", [128, 6, NB], BF16)
        PL = [sb("PL%d" % i, [128, 15 + NB]) for i in range(4)]
        MB = sb("MB", [128, 2, NB], BF16)
        KB = [sb("KB%d" % i, [128, TKV], BF16) for i in range(2)]
        rKB = Rot(KB, ["KB0", "KB1"])
        VB = [sb("VB%d" % i, [128, NKT, 65], BF16) for i in range(2)]
        rVB = Rot(VB, ["VB0", "VB1"])
        PT = [sb("PT%d" % i, [128, 2 * NB], BF16) for i in range(3)]
        rPT = Rot(PT, ["PT0", "PT1", "PT2"])
        OT = [sb("OT%d" % i, [65, 2 * NB]) for i in range(2)]
        rOT = Rot(OT, ["OT0", "OT1"])
        OM = [sb("OM%d" % i, [128, 2, 65]) for i in range(3)]
        RR = sb("RR", [128, 8]); DD = sb("DD", [128, 2, 64]); SSD = sb("SSD", [128, 2]); RSD = sb("RSD", [128, 2])
        MIXTOK = sb("MIXTOK", [128, 2, 768], BF16)
        WU = [sb("WU%d" % i, [128, 2, 2, 8, 128], BF16) for i in range(2)]
        rWU = Rot(WU, ["WU0", "WU1"])
        WD = [sb("WD%d" % i, [128, 2, 1024], BF16) for i in range(3)]
        rWD = Rot(WD, ["WD0", "WD1", "WD2"])
        GSB = [sb("GSB%d" % i, [128, NB + 2]) for i in range(2)]
        rGSB = Rot(GSB, ["GSB0", "GSB1"])
        AT = [sb("AT%d" % i, [128, NB]) for i in range(2)]
        rAT = Rot(AT, ["AT0", "AT1"])
        STt = [sb("ST%d" % i, [128, NB]) for i in range(2)]
        rST = Rot(STt, ["ST0", "ST1"])
        ACTB = [sb("ACTB%d" % i, [128, 2, NB], BF16) for i in range(3)]
        rACTB = Rot(ACTB, ["ACTB0", "ACTB1", "ACTB2"])
        GCAR = sb("GCAR", [128, NFC, 2])
        RPF = sb("RPF", [96, 2, NB]); RPT = sb("RPT", [128, 2, 2, 32])
        YTb = [sb("YT%d" % i, [128, NB]) for i in range(8)]
        rYT = Rot(YTb, ["YT%d" % i for i in range(8)])

        DVE.op(lambda e: e.memset(QDM[:], 0.0), [], [B("QDM")])
        DVE.op(lambda e: e.memset(QC[:], 0.0), [], [B("QC")])
        DVE.op(lambda e: e.memset(VST[:, :, :, 64:65], 1.0), [], [B("VST")])
        DVE.op(lambda e: e.memset(VCST[:, :, :, 64:65], 1.0), [], [B("VCST")])

        out_events = []

        def norm(N, A_ap, B_ap):
            XT = XTS[xsel[0]]; XTN = "XT%d" % xsel[0]
            actf(SQ[:, :, 0:N], XT[:, :, 0:N], AF.Square, [B(XTN)], [B("SQ")])
            ps, pb = rD.next()
            for c in range(8):
                mm(ps[:, 0:N], ones_bf[:, :], SQ[:, c, 0:N], c == 0, c == 7, [B("SQ"), B("ones_bf")], [pb])
            rstd_from(RSTD[:, 0:N], ps[:, 0:N], float(D), [pb], [B("RSTD")])
            for c in range(8):
                tm, tb = rTMP.next()
                tt(tm[:, 0:N], XT[:, c, 0:N], RSTD[:, 0:N], ALU.mult, [B(XTN), B("RSTD")], [tb])
                actf(HT[:, c, 0:N], tm[:, 0:N], AF.Identity, [tb, B("PRM"), B("VEC")], [B("HT")],
                     bias=B_ap[:, c:c + 1], scale=A_ap[:, c:c + 1])

        def mla_kv_from_latt(l, t0, N, rowsl, with_kr=True):
            kt0 = t0 // 128
            for ti, rw in enumerate(rowsl):
                ps, pb = rD.next()
                mm(ps[0:rw, 0:384], LATT[:, ti * 128:ti * 128 + rw], WKV[:, 384:768], True, True,
                   [B("LATT"), B("WKV")], [pb])
                cpy(VCST[0:rw, :, ti, 0:64], ps[0:rw, 0:384].rearrange("p (h d) -> p h d", d=64), [pb], [B("VCST")])
            for hh in range(6):
                ps, pb = rD.next()
                mm(ps[0:64, 0:N], WKV[:, hh * 64:(hh + 1) * 64], LATT[:, 0:N], True, True, [B("LATT"), B("WKV")], [pb])
                actf(KCST[0:64, hh, 0:N], ps[0:64, 0:N], AF.Copy, [pb], [B("KCST")])
                if with_kr:
                    cpy(KCST[64:96, hh, 0:N], KRST[64:96, 0:N], [B("KRST")], [B("KCST")])
            if with_kr:
                POOL.dma(KcT.ap()[:, :, t0:t0 + N], KCST[:, :, 0:N], [B("KCST")], [B("KcT")])
            else:
                POOL.dma(KcT.ap()[0:64, :, t0:t0 + N], KCST[0:64, :, 0:N], [B("KCST")], [B("KcT")])
            POOL.dma(Vcs.ap()[:, :, kt0:kt0 + len(rowsl), :], VCST[:, :, 0:len(rowsl), :], [B("VCST")], [B("Vcs")])

        def load_small_weights(l):
            SY.dma(WQ[:], wsc["w_q"].ap()[l].rearrange("p (a b) -> p a b", a=2), [B("ws_w_q_%d_0" % l)], [B("WQ")])
            SY.dma(WKV[:], wsc["w_kv"].ap()[l], [B("ws_w_kv_%d_0" % l)], [B("WKV")])
            SY.dma(WPOOL[:], wsc["w_pool"].ap()[l].rearrange("p (a b) -> p a b", a=2), [B("ws_w_pool_%d_0" % l)], [B("WPOOL")])

        def attn_pass(kind, l, h, N, t0, kt0, nkt, last_ksz, is_sample, kb, kbb, vb, vbb, bg_step, bg_flush):
            TTq = max(1, N // 128)
            nm = 2 if kind == "a" else 1
            acc, accb = rA.next()
            scale = SC_A if kind == "a" else SC_C
            def v3(t, rows, c0, st=NB):
                if nm == 1:
                    return t[rows, c0:N]
                return t[rows, 0:2 * st].rearrange("p (m n) -> p m n", m=2)[:, :, c0:N]
            def qk(kt):
                ksz = last_ksz if kt == nkt - 1 else 128
                c0 = 0 if is_sample else 128 * max(0, kt - kt0)
                s_, sbf = rS.next()
                ks = slice(kt * 128, kt * 128 + ksz)
                groups = []
                if kind == "a":
                    for m in range(2):
                        p0 = 64 * (h % 2) + 32 * m
                        g = [(s_[0:ksz, m * NB + c0:m * NB + N], kb[:, ks], QDM[:, h // 2, 2 * (h % 2) + m, c0:N], [kbb, B("QDM")], None)]
                        for jq in range(TTq):
                            o = kt0 + jq - kt
                            if o in (0, 1) and jq * 128 >= c0:
                                qw = min(128, N)
                                g.append((s_[0:ksz, m * NB + jq * 128:m * NB + jq * 128 + qw], ident_bf[0:ksz, 0:ksz],
                                          BIAST[0:ksz, o, h, 0:qw], [B("ident_bf"), B("BIAST")], None))
                        groups.append(g)
                else:
                    groups.append([(s_[0:ksz, c0:N], kb[:, ks], QC[:, h, c0:N], [kbb, B("QC")], None)])
                for g in groups:
                    for i in range(len(g)):
                        o_, a_, b_, rd, tp = g[i]
                        mm(o_, a_, b_, i == 0, i == len(g) - 1, rd, [sbf], tp=tp)
                return s_, sbf, ksz, c0
            def fin(kt, st):
                s_, sbf, ksz, c0 = st
                pt, ptb = rPT.next()
                rows = slice(0, ksz)
                if kind == "a":
                    actf(v3(pt, rows, c0), v3(s_, rows, c0), AF.Exp, [sbf, B("R15")], [ptb], bias=R15[0:ksz, h:h + 1], scale=scale)
                else:
                    actf(v3(pt, rows, c0), v3(s_, rows, c0), AF.Exp, [sbf], [ptb], scale=scale)
                if not is_sample and kt >= kt0:
                    jq = kt - kt0
                    if nm == 1:
                        msk = pt[64:128, jq * 128:jq * 128 + 64]
                    else:
                        msk = pt[64:128, 0:2 * NB].rearrange("p (m n) -> p m n", m=2)[:, :, jq * 128:jq * 128 + 64]
                    DVE.op(lambda e, a=msk: e.memset(a, 0.0), [], [ptb])
                mm(v3(acc, slice(0, 65), c0), vb[0:ksz, kt, :], v3(pt, rows, c0), kt == 0, kt == nkt - 1, [vbb, ptb], [accb])
            pend = []
            for kt in range(min(3, nkt)):
                pend.append((kt, qk(kt)))
            nxt = len(pend)
            done = 0
            yielded = False
            while pend:
                kt, st = pend.pop(0)
                fin(kt, st)
                if nxt < nkt:
                    pend.append((nxt, qk(nxt))); nxt += 1
                done += 1
                if done > 2:
                    bg_step(); bg_step()
                if done == 2 and not yielded:
                    yielded = True
                    yield None
            if not yielded:
                yield None
            bg_flush()
            def tail():
                rowsl = [128] * TTq if N >= 128 else [N]
                ot, otb = rOT.next()
                if nm == 1:
                    actf(ot[:, 0:N], acc[0:65, 0:N], AF.Copy, [accb], [otb])
                else:
                    actf(ot[:, 0:2 * NB].rearrange("p (m n) -> p m n", m=2)[:, :, 0:N],
                         acc[0:65, 0:2 * NB].rearrange("p (m n) -> p m n", m=2)[:, :, 0:N], AF.Copy, [accb], [otb])
                ps, pb = rD.next()
                for m in range(nm):
                    for ti, rw in enumerate(rowsl):
                        cc = (m * TTq + ti) * 65
                        mm(ps[0:rw, cc:cc + 65], ot[0:65, m * NB + ti * 128:m * NB + ti * 128 + rw], ident_f[0:65, 0:65], True, True,
                           [otb, B("ident_f")], [pb])
                for m in range(nm):
                    oi = m if kind == "a" else 2
                    for ti, rw in enumerate(rowsl):
                        cc = (m * TTq + ti) * 65
                        cpy(OM[oi][0:rw, ti, :], ps[0:rw, cc:cc + 65], [pb], [B("OM%d" % oi)])
            yield tail

        class _CkStop(Exception):
            pass
        def ck(n):
            if DEBUG_STOP is not None and DEBUG_STOP.get("ck") == n:
                raise _CkStop()

        def block(s, l, blk, N, t0, is_sample, is_last):
            TTq = max(1, N // 128)
            rowsl = [128] * TTq if N >= 128 else [N]
            kt0 = t0 // 128
            nkt = kt0 + TTq
            last_ksz = rowsl[-1]
            vo = l * VL
            P_ = lambda j: PRM[:, l, s, j, :]
            tq = t0 - (PAST if is_sample else 0)
            xsel[0] ^= 1
            XT = XTS[xsel[0]]; XTN = "XT%d" % xsel[0]
            if l == 0:
                SY.dma(XT[:, :, 0:N], xT[s].ap()[:, :, tq:tq + N], [], [B(XTN)])
            else:
                SY.dma(XT[:, :, 0:N], xscr.ap()[:, :, tq:tq + N], [B("xscr")], [B(XTN)])
            SY.dma(RPF[:, :, 0:N], ropeF.ap()[:, :, t0:t0 + N], [], [B("RPF")])
            SY.dma(RPT[:, 0:TTq, :, :], ropeT.ap()[:, kt0:kt0 + TTq, :, :], [], [B("RPT")])
            SY.dma(WT[:], wsc["w_inT"].ap()[l].rearrange("p (a b) -> p a b", a=8), [B("ws_w_inT_%d_0" % l)], [B("WT")])
            norm(N, P_(0), P_(1))
            ck(1)
            pend_kr = None
            for j in range(13):
                wf, wfb = rWF.next()
                SY.dma(wf[:], wsc["w_inF"].ap()[l, j * 128:(j + 1) * 128, :].rearrange("p (a b) -> p a b", a=8),
                       [B("ws_w_inF_%d_%d" % (l, j))], [wfb])
                M = 96 if j >= 11 else 128
                ps, pb = rD.next()
                for kc in range(8):
                    mm(ps[0:M, 0:N], wf[:, kc, 0:M], HT[:, kc, 0:N], kc == 0, kc == 7, [wfb, B("HT")], [pb])
                if j < 3:
                    for hm in range(4):
                        actf(QDM[32 * hm:32 * hm + 32, j, hm, 0:N], ps[32 * hm:32 * hm + 32, 0:N], AF.Copy, [pb], [B("QDM")])
                elif j < 6:
                    actf(KST[:, j - 3, 0:N], ps[:, 0:N], AF.Copy, [pb], [B("KST")])
                elif j < 8:
                    cpy(UB[:, j - 6, 15:15 + N], ps[:, 0:N], [pb], [B("UB")])
                elif j < 10:
                    c = j - 8
                    actf(CQG[:, c, 0:N], ps[:, 0:N], AF.Identity, [pb, B("VEC")], [B("CQG")], scale=VEC[:, vo + 18 + c:vo + 19 + c])
                    actf(CQSQ[:, c, 0:N], ps[:, 0:N], AF.Square, [pb], [B("CQSQ")])
                elif j == 10:
                    actf(CKVSQ[:, 0:N], ps[:, 0:N], AF.Square, [pb], [B("CKVSQ")])
                    ts(CKVG[:, 0:N], ps[:, 0:N], VEC[:, vo + 110:vo + 111], None, ALU.mult, None, [pb, B("VEC")], [B("CKVG")])
                elif j == 11:
                    tt(T1[0:96, 0:N], ps[0:96, 0:N], RPF[:, 0, 0:N], ALU.mult, [pb, B("RPF")], [B("T1")])
                else:
                    tt(T2[0:96, 0:N], ps[0:96, 0:N], RPF[:, 1, 0:N], ALU.mult, [pb, B("RPF")], [B("T2")])
                    tt(KRST[:, 0:N], T1[0:96, 0:N], T2[0:96, 0:N], ALU.add, [B("T1"), B("T2")], [B("KRST")])
                ck(100 + j)
            ck(2)
            ps, pb = rD.next()
            for c in range(2):
                mm(ps[:, 0:N], ones_bf[:, :], CQSQ[:, c, 0:N], c == 0, c == 1, [B("CQSQ"), B("ones_bf")], [pb])
            rstd_from(RSTDQ[:, 0:N], ps[:, 0:N], 256.0, [pb], [B("RSTDQ")])
            ps, pb = rD.next()
            mm(ps[:, 0:N], ones_bf[:, :], CKVSQ[:, 0:N], True, True, [B("CKVSQ"), B("ones_bf")], [pb])
            rstd_from(RSTDKV[:, 0:N], ps[:, 0:N], 128.0, [pb], [B("RSTDKV")])
            tt(LATT[:, 0:N], CKVG[:, 0:N], RSTDKV[:, 0:N], ALU.mult, [B("CKVG"), B("RSTDKV")], [B("LATT")])
            POOL.dma(KdT.ap()[:, :, t0:t0 + N], KST[:, :, 0:N], [B("KST")], [B("KdT")])
            ck(3)
            for ti, rw in enumerate(rowsl):
                tk = slice(ti * 128, ti * 128 + rw)
                psA, pbA = rD.next()
                for kc in range(8):
                    mm(psA[0:rw, 0:512], HT[:, kc, tk], WT[:, kc, 0:512], kc == 0, kc == 7, [B("HT"), B("WT")], [pbA])
                psB, pbB = rD.next()
                for kc in range(8):
                    mm(psB[0:rw, 0:448], HT[:, kc, tk], WT[:, kc, 512:960], kc == 0, kc == 7, [B("HT"), B("WT")], [pbB])
                osb, osbb = rOSB.next()
                actf(osb[0:rw, 0:384], psA[0:rw, 0:384], AF.Copy, [pbA], [osbb])
                actf(JUNK[0:rw, :], psA[0:rw, 384:512], AF.Square, [pbA], [B("JUNK")])
                DVE.op(lambda e, o=SSL[0:rw, 0:1], i=JUNK[0:rw, :]: e.reduce_sum(out=o, in_=i, axis=AX.X), [B("JUNK")], [B("SSL")])
                rstd_from(SSL[0:rw, 1:2], SSL[0:rw, 0:1], 128.0, [B("SSL")], [B("SSL")])
                stt(osb[0:rw, 768:896], psA[0:rw, 384:512], SSL[0:rw, 1:2], ROWS[0:rw, l * 192:l * 192 + 128], ALU.mult, ALU.mult,
                    [pbA, B("SSL"), B("ROWS")], [osbb])
                actf(osb[0:rw, 384:768], psB[0:rw, 0:384], AF.Copy, [pbB], [osbb])
                cpy(VST[0:rw, :, ti, 0:64], psB[0:rw, 0:384].rearrange("p (h d) -> p h d", d=64), [pbB], [B("VST")])
                tt(KT1[0:rw, :], psB[0:rw, 384:416], RPT[0:rw, ti, 0, :], ALU.mult, [pbB, B("RPT")], [B("KT1")])
                tt(KT2[0:rw, :], psB[0:rw, 416:448], RPT[0:rw, ti, 1, :], ALU.mult, [pbB, B("RPT")], [B("KT2")])
                tt(osb[0:rw, 896:928], KT1[0:rw, :], KT2[0:rw, :], ALU.add, [B("KT1"), B("KT2")], [osbb])
                r0 = t0 - (PAST if is_sample else 0) + ti * 128
                out_events.append(POOL.dma(o_ak[s].ap()[l, r0:r0 + rw, :], osb[0:rw, 0:384], [osbb], []))
                out_events.append(POOL.dma(o_av[s].ap()[l, r0:r0 + rw, :], osb[0:rw, 384:768], [osbb], []))
                out_events.append(POOL.dma(o_lat[s].ap()[l, r0:r0 + rw, :], osb[0:rw, 768:896], [osbb], []))
                out_events.append(POOL.dma(o_kr[s].ap()[l, r0:r0 + rw, :], osb[0:rw, 896:928], [osbb], []))
            POOL.dma(Vds.ap()[:, :, kt0:kt0 + TTq, :], VST[:, :, 0:TTq, :], [B("VST")], [B("Vds")])
            ck(4)
            mla_kv_from_latt(l, t0, N, rowsl)
            ck(5)
            for hh in range(6):
                ps, pb = rD.next()
                for kc in range(2):
                    mm(ps[0:96, 0:N], WQ[:, kc, hh * 96:(hh + 1) * 96], CQG[:, kc, 0:N], kc == 0, kc == 1, [B("WQ"), B("CQG")], [pb])
                ps2, pb2 = rD.next()
                for kc in range(2):
                    mm(ps2[64:96, 0:N], WQ[:, kc, 576 + hh * 32:576 + (hh + 1) * 32], CQG[:, kc, 0:N], kc == 0, kc == 1,
                       [B("WQ"), B("CQG")], [pb2], tp=(0, 64))
                tt(QC[0:64, hh, 0:N], ps[0:64, 0:N], RSTDQ[0:64, 0:N], ALU.mult, [pb, B("RSTDQ")], [B("QC")])
                tt(T1[64:96, 0:N], ps[64:96, 0:N], RPF[64:96, 0, 0:N], ALU.mult, [pb, B("RPF")], [B("T1")])
                tt(T2[64:96, 0:N], ps2[64:96, 0:N], RPF[64:96, 1, 0:N], ALU.mult, [pb2, B("RPF")], [B("T2")])
                tt(T1[64:96, 0:N], T1[64:96, 0:N], T2[64:96, 0:N], ALU.add, [B("T1"), B("T2")], [B("T1")])
                tt(QC[64:96, hh, 0:N], T1[64:96, 0:N], RSTDQ[64:96, 0:N], ALU.mult, [B("T1"), B("RSTDQ")], [B("QC")])
            ck(6)
            L = 15 + N
            for c in range(2):
                u = UB[:, c, :]
                tt(PL[0][:, 1:L], u[:, 1:L], u[:, 0:L - 1], ALU.add, [B("UB")], [B("PL0")])
                tt(PL[1][:, 3:L], PL[0][:, 3:L], PL[0][:, 1:L - 2], ALU.add, [B("PL0")], [B("PL1")])
                if c == 0:
                    srcs = [(PL[0], "PL0"), (PL[1], "PL1")]
                else:
                    tt(PL[2][:, 7:L], PL[1][:, 7:L], PL[1][:, 3:L - 4], ALU.add, [B("PL1")], [B("PL2")])
                    tt(PL[3][:, 15:L], PL[2][:, 15:L], PL[2][:, 7:L - 8], ALU.add, [B("PL2")], [B("PL3")])
                    srcs = [(PL[2], "PL2"), (PL[3], "PL3")]
                for hf in range(2):
                    pr = slice(hf * 64, hf * 64 + 64)
                    src, sn = srcs[hf]
                    if (not is_sample) and blk == 0:
                        tt(T1[pr, 0:N], src[pr, 15:L], INVC0[pr, c, 0:N], ALU.mult, [B(sn), B("INVC0")], [B("T1")])
                        tt(MB[pr, c, 0:N], T1[pr, 0:N], UB[pr, c, 15:L], ALU.subtract, [B("T1"), B("UB")], [B("MB")])
                    else:
                        stt(MB[pr, c, 0:N], src[pr, 15:L], VEC[pr, vo + 108 + c:vo + 109 + c], UB[pr, c, 15:L],
                            ALU.mult, ALU.subtract, [B(sn), B("VEC"), B("UB")], [B("MB")])
                ps, pb = rD.next()
                mm(ps[:, 0:N], WPOOL[:, c, :], MB[:, c, 0:N], True, True, [B("WPOOL"), B("MB")], [pb])
                actf(MIXT[:, 3 + c, 0:N], ps[:, 0:N], AF.Identity, [pb, B("VEC")], [B("MIXT")], scale=VEC[:, vo + 16 + c:vo + 17 + c])
            if is_last:
                out_events.append(POOL.dma(o_pool[s].ap()[l], UB[:, :, N:N + 15], [B("UB")], []))
            cpy(UB[:, :, 0:15], UB[:, :, N:N + 15], [B("UB")], [B("UB")])
            ck(7)
            gsub = ROWS[:, l * 192 + 128:l * 192 + 192]
            lam_init = 0.8 - 0.6 * math.exp(-0.3 * l)
            pending_tail = [None]
            def head_post(kind, h):
                for ti, rw in enumerate(rowsl):
                    if kind == "a":
                        DVE.op(lambda e, o=RR[0:rw, 0:1], i=OM[0][0:rw, ti, 64:65]: e.reciprocal(out=o, in_=i), [B("OM0")], [B("RR")])
                        yield
                        DVE.op(lambda e, o=RR[0:rw, 1:2], i=OM[1][0:rw, ti, 64:65]: e.reciprocal(out=o, in_=i), [B("OM1")], [B("RR")])
                        yield
                        tt(RR[0:rw, 1:2], RR[0:rw, 1:2], LAM[0:rw, 2 + l:3 + l], ALU.mult, [B("RR"), B("LAM")], [B("RR")])
                        yield
                        ts(DD[0:rw, 0, :], OM[0][0:rw, ti, 0:64], RR[0:rw, 0:1], None, ALU.mult, None, [B("OM0"), B("RR")], [B("DD")])
                        yield
                        stt(DD[0:rw, 1, :], OM[1][0:rw, ti, 0:64], RR[0:rw, 1:2], DD[0:rw, 0, :], ALU.mult, ALU.add,
                            [B("OM1"), B("RR"), B("DD")], [B("DD")])
                        yield
                        tt(JUNK[0:rw, 0:64], DD[0:rw, 1, :], DD[0:rw, 1, :], ALU.mult, [B("DD")], [B("JUNK")])
                        yield
                        DVE.op(lambda e, o=SSD[0:rw, 0:1], i=JUNK[0:rw, 0:64]: e.reduce_sum(out=o, in_=i, axis=AX.X), [B("JUNK")], [B("SSD")])
                        yield
                        actf(SSD[0:rw, 1:2], SSD[0:rw, 0:1], AF.Ln, [B("SSD")], [B("SSD")], bias=EPS, scale=1.0 / 64.0)
                        yield
                        actf(SSD[0:rw, 1:2], SSD[0:rw, 1:2], AF.Exp, [B("SSD")], [B("SSD")], scale=-0.5)
                        yield
                        ts(SSD[0:rw, 1:2], SSD[0:rw, 1:2], 1.0 - lam_init, None, ALU.mult, None, [B("SSD")], [B("SSD")])
                        yield
                        stt(MIXTOK[0:rw, ti, h * 64:h * 64 + 64], DD[0:rw, 1, :], SSD[0:rw, 1:2], gsub[0:rw, :], ALU.mult, ALU.mult,
                            [B("DD"), B("SSD"), B("ROWS")], [B("MIXTOK")])
                        yield
                    else:
                        DVE.op(lambda e, o=RR[0:rw, 2:3], i=OM[2][0:rw, ti, 64:65]: e.reciprocal(out=o, in_=i), [B("OM2")], [B("RR")])
                        yield
                        ts(MIXTOK[0:rw, ti, 384 + h * 64:384 + h * 64 + 64], OM[2][0:rw, ti, 0:64], RR[0:rw, 2:3], None, ALU.mult, None,
                           [B("OM2"), B("RR")], [B("MIXTOK")])
                        yield
            bg = []
            def bg_step():
                while bg:
                    try:
                        next(bg[0])
                        return
                    except StopIteration:
                        bg.pop(0)
            def bg_flush():
                while bg:
                    bg_step()

            for kind in ("a", "c"):
                Ksc, KSTG, KSN = (KdT, KST, "KST") if kind == "a" else (KcT, KCST, "KCST")
                Vsc, VSTG, VSN = (Vds, VST, "VST") if kind == "a" else (Vcs, VCST, "VCST")
                KscN = "KdT" if kind == "a" else "KcT"
                VscN = "Vds" if kind == "a" else "Vcs"
                kb = kbb = None
                for h in range(6):
                    if kind == "a" and h % 2 == 0:
                        kb, kbb = rKB.next()
                        if t0 > 0:
                            SY.dma(kb[:, 0:t0], Ksc.ap()[:, h // 2, 0:t0], [B(KscN)], [kbb])
                        cpy(kb[:, t0:t0 + N], KSTG[:, h // 2, 0:N], [B(KSN)], [kbb])
                    elif kind == "c":
                        kb, kbb = rKB.next()
                        if t0 > 0:
                            SY.dma(kb[0:96, 0:t0], Ksc.ap()[:, h, 0:t0], [B(KscN)], [kbb])
                        cpy(kb[0:96, t0:t0 + N], KSTG[:, h, 0:N], [B(KSN)], [kbb])
                    vb, vbb = rVB.next()
                    if kt0 > 0:
                        SY.dma(vb[:, 0:kt0, :], Vsc.ap()[:, h, 0:kt0, :], [B(VscN)], [vbb])
                    cpy(vb[:, kt0:kt0 + TTq, :], VSTG[:, h, 0:TTq, :], [B(VSN)], [vbb])
                    g = attn_pass(kind, l, h, N, t0, kt0, nkt, last_ksz, is_sample, kb, kbb, vb, vbb, bg_step, bg_flush)
                    next(g)
                    if pending_tail[0] is not None:
                        pending_tail[0]()
                    this_tail = next(g)
                    def full_tail(this_tail=this_tail, kind=kind, h=h):
                        this_tail()
                        bg.append(head_post(kind, h))
                    pending_tail[0] = full_tail
            if pending_tail[0] is not None:
                pending_tail[0]()
                pending_tail[0] = None
            bg_flush()
            for ch in range(6):
                ps, pb = rD.next()
                for ti, rw in enumerate(rowsl):
                    mm(ps[:, ti * 128:ti * 128 + rw], MIXTOK[0:rw, ti, ch * 128:(ch + 1) * 128], ident_bf[0:rw, 0:rw], True, True,
                       [B("MIXTOK"), B("ident_bf")], [pb])
                mc = ch if ch < 3 else ch + 2
                actf(MIXT[:, mc, 0:N], ps[:, 0:N], AF.Copy, [pb], [B("MIXT")])
            ck(9)
            for mc in range(8):
                wf, wfb = rWF.next()
                SY.dma(wf[:], wsc["w_out"].ap()[l, mc * 128:(mc + 1) * 128, :].rearrange("p (a b) -> p a b", a=8),
                       [B("ws_w_out_%d_%d" % (l, mc))], [wfb])
                ps, pb = rD.next()
                for kc in range(8):
                    mm(ps[:, 0:N], wf[:, kc, :], MIXT[:, kc, 0:N], kc == 0, kc == 7, [wfb, B("MIXT")], [pb])
                stt(XT[:, mc, 0:N], ps[:, 0:N], PRM[:, l, s, 2, mc:mc + 1], XT[:, mc, 0:N], ALU.mult, ALU.add,
                    [pb, B("PRM"), B(XTN)], [B(XTN)])
            ck(10)
            norm(N, P_(3), P_(4))
            for dps, dpb in DACC:
                for bk in range(2):
                    mm(dps[:, bk * 512:(bk + 1) * 512], ZR[:, 0:128], ZR[:, 0:512], True, False, [B("ZR")], dpb)
            def down_group(grp, wd, wdb, ab, abb):
                for mc in range(8):
                    dps, dpb = DACC[mc // 4]
                    for f in range(2):
                        mm(dps[:, (mc % 4) * NB:(mc % 4) * NB + N], wd[:, f, mc * 128:(mc + 1) * 128], ab[:, f, 0:N],
                           False, grp == 10 and f == 1, [wdb, abb], dpb)
            prev_down = None
            for grp in range(11):
                wu, wub = rWU.next(); wd, wdb = rWD.next()
                SY.dma(wu[:], wsc["w_up"].ap()[l, grp * 128:(grp + 1) * 128, :].rearrange("p (a b c d) -> p a b c d", a=2, b=2, c=8),
                       [B("ws_w_up_%d_%d" % (l, grp))], [wub])
                SY.dma(wd[:], wsc["w_down"].ap()[l, grp * 128:(grp + 1) * 128, :].rearrange("p (a b) -> p a b", a=2),
                       [B("ws_w_down_%d_%d" % (l, grp))], [wdb])
                ab, abb = rACTB.next()
                for f in range(2):
                    fc = grp * 2 + f
                    gps, gpb = rF.next()
                    for kc in range(8):
                        mm(gps[:, 0:N], wu[:, f, 0, kc, :], HT[:, kc, 0:N], kc == 0, kc == 7, [wub, B("HT")], [gpb])
                    vps, vpb = rF.next()
                    for kc in range(8):
                        mm(vps[:, 0:N], wu[:, f, 1, kc, :], HT[:, kc, 0:N], kc == 0, kc == 7, [wub, B("HT")], [vpb])
                    g, gb = rGSB.next()
                    cw = vo + 20
                    a, ab_ = rAT.next()
                    actf(g[:, 2:2 + N], gps[:, 0:N], AF.Copy, [gpb], [gb])
                    actf(a[:, 0:N], gps[:, 0:N], AF.Identity, [gpb, B("VEC")], [ab_],
                         bias=VEC[:, vo + 86 + fc:vo + 87 + fc], scale=VEC[:, cw + 44 + fc:cw + 45 + fc])
                    cpy(g[:, 0:2], GCAR[:, fc, :], [B("GCAR")], [gb])
                    cpy(GCAR[:, fc, :], g[:, N:N + 2], [gb], [B("GCAR")])
                    stt(a[:, 0:N], g[:, 1:1 + N], VEC[:, cw + 22 + fc:cw + 23 + fc], a[:, 0:N], ALU.mult, ALU.add, [gb, B("VEC"), ab_], [ab_])
                    stt(a[:, 0:N], g[:, 0:N], VEC[:, cw + fc:cw + 1 + fc], a[:, 0:N], ALU.mult, ALU.add, [gb, B("VEC"), ab_], [ab_])
                    st, stb = rST.next()
                    actf(st[:, 0:N], a[:, 0:N], AF.Silu, [ab_], [stb])
                    tt(ab[:, f, 0:N], st[:, 0:N], vps[:, 0:N], ALU.mult, [stb, vpb], [abb])
                if prev_down is not None:
                    down_group(*prev_down)
                prev_down = (grp, wd, wdb, ab, abb)
            down_group(*prev_down)
            for mc in range(8):
                dps, dpb = DACC[mc // 4]
                stt(XT[:, mc, 0:N], dps[:, (mc % 4) * NB:(mc % 4) * NB + N], PRM[:, l, s, 5, mc:mc + 1], XT[:, mc, 0:N], ALU.mult, ALU.add,
                    dpb + [B("PRM"), B(XTN)], [B(XTN)])
            if is_last:
                out_events.append(POOL.dma(o_conv[s].ap()[l], GCAR[:], [B("GCAR")], []))
            ck(11)
            if l == 0:
                POOL.dma(xscr.ap()[:, :, tq:tq + N], XT[:, :, 0:N], [B(XTN)], [B("xscr")])
            else:
                actf(SQ[:, :, 0:N], XT[:, :, 0:N], AF.Square, [B(XTN)], [B("SQ")])
                ps, pb = rD.next()
                for c in range(8):
                    mm(ps[:, 0:N], ones_bf[:, :], SQ[:, c, 0:N], c == 0, c == 7, [B("SQ"), B("ones_bf")], [pb])
                rstd_from(RSTD[:, 0:N], ps[:, 0:N], float(D), [pb], [B("RSTD")])
                for c in range(8):
                    y, yb = rYT.next()
                    stt(y[:, 0:N], XT[:, c, 0:N], VEC[:, 2 * VL + c:2 * VL + c + 1], RSTD[:, 0:N], ALU.mult, ALU.mult,
                        [B(XTN), B("VEC"), B("RSTD")], [yb])
                    out_events.append(POOL.dma(yT[s].ap()[:, c, tq:tq + N], y[:, 0:N], [yb], []))

        class _Stop(Exception):
            pass
        nblocks_emitted = [0]
        _orig_block = block
        def block(*a_, **k_):
            if DEBUG_STOP is not None and nblocks_emitted[0] >= DEBUG_STOP.get("nblocks", 10 ** 9):
                return
            nblocks_emitted[0] += 1
            try:
                _orig_block(*a_, **k_)
            except _CkStop:
                pass
        for s in (DEBUG_STOP.get("seqs", range(3)) if DEBUG_STOP else range(3)):
            is_sample = (s == 2)
            for l in (DEBUG_STOP.get("layers", range(DEPTH)) if DEBUG_STOP else range(DEPTH)):
                load_small_weights(l)
                if is_sample:
                    SY.dma(UB[:, :, 0:15], poolst.ap()[l], [], [B("UB")])
                    SY.dma(GCAR[:], convst.ap()[l], [], [B("GCAR")])
                    for j_ in range(3):
                        POOL.dma(KdT.ap()[:, j_, 0:PAST], cak.ap()[l, :, j_, :], [], [B("KdT")])
                    for h_ in range(6):
                        POOL.dma(Vds.ap()[:, h_, 0:32, :], cav.ap()[l, :, h_, :, :], [], [B("Vds")])
                    for h_ in range(6):
                        POOL.dma(KcT.ap()[64:96, h_, 0:PAST], ckr.ap()[l, 64:96, :], [], [B("KcT")])
                    for pc in range(PAST // NB):
                        POOL.dma(LATT[:, 0:NB], clat.ap()[l, :, pc * NB:(pc + 1) * NB], [], [B("LATT")])
                        mla_kv_from_latt(l, pc * NB, NB, [128, 128], with_kr=False)
                    block(s, l, 0, TS, PAST, True, True)
                else:
                    DVE.op(lambda e: e.memset(UB[:, :, 0:15], 0.0), [], [B("UB")])
                    DVE.op(lambda e: e.memset(GCAR[:], 0.0), [], [B("GCAR")])
                    nblk = T // NB
                    for blk in range(nblk):
                        block(s, l, blk, NB, blk * NB, False, blk == nblk - 1)
        for ev in out_events:
            SY.wait(ev)
        STATS.update({e_.name: (e_.sem.cnt if e_.sem else 0, len(e_.ops)) for e_ in R.engs})
        STATS["dma"] = sum(x.cnt for x in R.dsems)
        with nc.Block() as blockctx:
            R.replay(blockctx)
    return nc


def _prep_weights(inp):
    sw = np.concatenate([np.arange(16, 32), np.arange(16)])
    AQ = np.arange(0, 384); AK = np.arange(384, 768); AV = np.arange(768, 1152); Bc = np.arange(1152, 1408)
    CQ = np.arange(1408, 1664); CKV = np.arange(1664, 1792); CKR = np.arange(1792, 1824)
    def pad128(ix):
        return np.concatenate([ix, np.repeat(ix[:1], 128 - len(ix))]) if len(ix) < 128 else ix
    fchunks = [AQ[0:128], AQ[128:256], AQ[256:384], AK[0:128], AK[128:256], AK[256:384], Bc[0:128], Bc[128:256],
               CQ[0:128], CQ[128:256], CKV, pad128(np.tile(CKR, 3)), pad128(np.tile(CKR[sw], 3))]
    tcols = np.concatenate([AK, CKV, AV, CKR, CKR[sw]])
    out = {k: [] for k in ("w_inF", "w_inT", "w_q", "w_kv", "w_pool", "w_out", "w_up", "w_down")}
    for l in range(DEPTH):
        w_in = np.asarray(inp["w_in"][l])
        out["w_inF"].append(np.stack([_kchunk(w_in[:, ix]) for ix in fchunks]).reshape(13 * 128, 8 * 128))
        out["w_inT"].append(_kchunk(w_in[:, tcols]).reshape(128, 8 * 960))
        wq = np.asarray(inp["w_q_up"][l])
        qcols = [np.arange(576)] + [hh * 96 + 64 + sw for hh in range(6)]
        out["w_q"].append(_kchunk(wq[:, np.concatenate(qcols)]).reshape(128, 2 * 768))
        wkv = np.asarray(inp["w_kv_up"][l])
        kcols = [hh * 128 + np.arange(64) for hh in range(6)] + [hh * 128 + 64 + np.arange(64) for hh in range(6)]
        out["w_kv"].append(np.ascontiguousarray(wkv[:, np.concatenate(kcols)]))
        pw = np.asarray(inp["pool_w"][l])
        wp = np.zeros((128, 2, 128), np.float32)
        for c in range(2):
            wp[0:64, c, 0:64] = pw[2 * c]
            wp[64:128, c, 64:128] = pw[2 * c + 1]
        out["w_pool"].append(wp.reshape(128, 256))
        wo = np.asarray(inp["w_out"][l])
        out["w_out"].append(np.stack([_kchunk(wo[:, mc * 128:(mc + 1) * 128]) for mc in range(8)]).reshape(8 * 128, 8 * 128))
        wu = np.asarray(inp["w_up"][l])
        wuk = _kchunk(wu)
        wug = wuk.reshape(128, 8, 2, 11, 2, 128)
        out["w_up"].append(np.ascontiguousarray(wug.transpose(3, 0, 4, 2, 1, 5)).reshape(11 * 128, 4096))
        wd = np.asarray(inp["w_down"][l])
        wdg = wd.reshape(11, 2, 128, 1024)
        out["w_down"].append(np.ascontiguousarray(wdg.transpose(0, 2, 1, 3)).reshape(11 * 128, 2048))
    return {k: np.stack(v).astype(np.float32) for k, v in out.items()}


def prep_inputs(inp):
    inp = {k: np.asarray(v) for k, v in inp.items()}
    ct = _const_tables()
    W = _prep_weights(inp)
    vecs = np.zeros((128, 2 * VL + 8), np.float32)
    for l in range(DEPTH):
        o = l * VL
        vecs[:, o + 0:o + 8] = _fm(inp["g_mix"][l])
        vecs[:, o + 8:o + 16] = _fm(inp["g_ffn"][l])
        vecs[:, o + 16:o + 18] = _fm(inp["pool_scale"][l])
        vecs[:, o + 18:o + 20] = _fm(inp["c_q_norm_g"][l])
        for j in range(3):
            vecs[:, o + 20 + j * 22:o + 20 + (j + 1) * 22] = _fm(inp["conv_w"][l, j])
        vecs[:, o + 86:o + 108] = _fm(inp["conv_b"][l])
        vecs[:, o + 108:o + 110] = ct["invw"]
        vecs[:, o + 110:o + 111] = _fm(inp["c_kv_norm_g"][l])
    for l in range(DEPTH):
        vecs[:, l * VL + 112:l * VL + 160] = _fm(inp["b_ada"][l])
    vecs[:, 2 * VL:2 * VL + 8] = _fm(inp["g_final"])
    rows = np.zeros((128, DEPTH * 192), np.float32)
    lamv = np.zeros((128, DEPTH * 4 * 32), np.float32)
    for l in range(DEPTH):
        rows[:, l * 192:l * 192 + 128] = inp["c_kv_norm_g"][l][None, :]
        rows[:, l * 192 + 128:l * 192 + 192] = inp["a_subln_g"][l][None, :]
        for i, k in enumerate(("lam_q1", "lam_k1", "lam_q2", "lam_k2")):
            lamv[:, (l * 4 + i) * 32:(l * 4 + i + 1) * 32] = inp[k][l][None, :]
    wada = np.stack([_kchunk(inp["w_ada"][l]) for l in range(DEPTH)])
    common = dict(wada=wada, rows=rows, lamv=lamv, relb=np.ascontiguousarray(inp["rel_bias"]),
                  r15c=np.ascontiguousarray(inp["rel_bias"][15][:, None]),
                  r15r=np.ascontiguousarray(np.broadcast_to(inp["rel_bias"][15][None, :], (128, 6))),
                  ohv=ct["ohv"], ropeF=ct["ropeF"], ropeT=ct["ropeT"], invcnt0=ct["invcnt0"], ident=ct["ident"], jmat=ct["jmat"])
    common.update(W)
    in_maps = []
    for c in range(8):
        m = dict(common)
        m["vecs"] = vecs
        for i, b in enumerate((2 * c, 2 * c + 1)):
            m["xT%d" % i] = np.ascontiguousarray(inp["x_prompt"][b].reshape(T, 8, 128).transpose(2, 1, 0))
        m["xT2"] = np.ascontiguousarray(inp["x_sample"][c].reshape(TS, 8, 128).transpose(2, 1, 0))
        cs = np.stack([inp["c_prompt"][2 * c], inp["c_prompt"][2 * c + 1], inp["c_sample"][c]])
        m["cT"] = np.ascontiguousarray(cs.reshape(3, 8, 128).transpose(2, 1, 0))
        m["poolst"] = np.ascontiguousarray(inp["state_b_pool"][:, c].reshape(DEPTH, 15, 2, 128).transpose(0, 3, 2, 1))
        m["convst"] = np.ascontiguousarray(inp["state_ffn_conv"][:, c].reshape(DEPTH, 2, NFC, 128).transpose(0, 3, 2, 1))
        ck = inp["cache_a_k"][:, c].reshape(DEPTH, PAST, 3, 128)
        m["cak"] = np.ascontiguousarray(ck.transpose(0, 3, 2, 1))
        cv = inp["cache_a_v"][:, c].reshape(DEPTH, 32, 128, 6, 64)
        cvo = np.ones((DEPTH, 128, 6, 32, 65), np.float32)
        cvo[..., 0:64] = cv.transpose(0, 2, 3, 1, 4)
        m["cav"] = cvo
        m["clat"] = np.ascontiguousarray(inp["cache_c_latent"][:, c].transpose(0, 2, 1))
        kr = inp["cache_c_krope"][:, c].transpose(0, 2, 1)
        m["ckr"] = np.ascontiguousarray(np.concatenate([kr, kr, kr], axis=1))
        in_maps.append(m)
    return in_maps


def kernel(**inp):
    in_maps = prep_inputs(inp)
    nc = build_program()
    res = run_bass_kernel_spmd(nc, in_maps, core_ids=list(range(8)))
    return assemble(res.results)


def assemble(rs):
    def unT(a):
        return np.ascontiguousarray(a.transpose(2, 1, 0).reshape(a.shape[2], D))
    y_p = np.stack([unT(rs[b // 2]["yT%d" % (b % 2)]) for b in range(16)])
    y_s = np.stack([unT(rs[c]["yT2"]) for c in range(8)])
    def gp(name):
        return np.stack([rs[b // 2]["%s%d" % (name, b % 2)] for b in range(16)], axis=1)
    def gs(name):
        return np.stack([rs[c]["%s2" % name] for c in range(8)], axis=1)
    def poolfix(a):
        return np.ascontiguousarray(a.transpose(0, 1, 4, 3, 2).reshape(a.shape[0], a.shape[1], 15, 256))
    def convfix(a):
        return np.ascontiguousarray(a.transpose(0, 1, 4, 3, 2).reshape(a.shape[0], a.shape[1], 2, FFN))
    outs = (y_p, y_s,
            gp("ak").reshape(DEPTH, 16, T, 6, 64), gp("av").reshape(DEPTH, 16, T, 6, 64), gp("lat"), gp("kr"),
            poolfix(gp("pool")), convfix(gp("conv")),
            gs("ak").reshape(DEPTH, 8, TS, 6, 64), gs("av").reshape(DEPTH, 8, TS, 6, 64), gs("lat"), gs("kr"),
            poolfix(gs("pool")), convfix(gs("conv")))
    return tuple(np.ascontiguousarray(o, dtype=np.float32) for o in outs)
```
